# Optimizing a Trainium2 kernel written in Bass

```python
import jax
import jax.numpy as jnp
from jax import lax
import numpy as np

D_MODEL = 1024
BATCH = 16
SEQ = 2048
DEPTH = 2

N_AB = (DEPTH + 1) // 2
N_CD = DEPTH // 2
D_FF = 4 * D_MODEL
EPS = 1e-5
SHORT_CONV = 3

RW_HEADS = 8
RW_HEAD = 64
RW_DIM = RW_HEADS * RW_HEAD
RW_DECAY_RANK = 64
RW_AAA_RANK = 64
RW_GATE_RANK = 128
RW_COLS = 3 * RW_DIM + RW_DECAY_RANK + RW_AAA_RANK + RW_GATE_RANK
RW_GN_EPS = 64e-5

MB_HEADDIM = 64
MB_HEADS = 16
MB_DIM = MB_HEADS * MB_HEADDIM
MB_GROUPS = 2
MB_HPG = MB_HEADS // MB_GROUPS
MB_STATE = 128
MB_CHUNK = 128
MB_XBC = MB_DIM + 2 * MB_GROUPS * MB_STATE
MB_COLS = MB_DIM + MB_XBC + 2 * MB_HEADS

AB_IN = RW_COLS + MB_COLS
AB_OUT = RW_DIM + MB_DIM

S5_GROUP = 16
S5_GROUPS = 32
S5_DIM = S5_GROUP * S5_GROUPS
S5_STATE = 64

ML_HEADS = 8
ML_HEAD = 128
ML_DIM = ML_HEADS * ML_HEAD
ML_BLOCK = 4
ML_CHUNK = 64
ML_COLS = 2 * ML_DIM + 4 * ML_HEADS

CD_IN = S5_DIM + ML_COLS
CD_OUT = S5_DIM + ML_DIM

kernel_name = "hybrid_bidir_rwkv7_mamba2_s5_mlstm"


def _split(t, sizes):
    return jnp.split(t, np.cumsum(sizes)[:-1].tolist(), axis=-1)


def rmsnorm(x, w):
    xf = x.astype(jnp.float32)
    y = xf * lax.rsqrt(jnp.mean(xf * xf, axis=-1, keepdims=True) + EPS)
    return (y * w.astype(jnp.float32)).astype(x.dtype)


def head_norm(x, w, eps):
    xf = x.astype(jnp.float32)
    xc = xf - jnp.mean(xf, axis=-1, keepdims=True)
    y = xc * lax.rsqrt(jnp.mean(xc * xc, axis=-1, keepdims=True) + eps)
    return y.reshape(x.shape[:-2] + (-1,)) * w.astype(jnp.float32)


def to_dirs(t):
    return jnp.stack([t, jnp.flip(t, axis=1)])


def flip_dir1(t):
    return jnp.stack([t[0], jnp.flip(t[1], axis=1)])


def merge_dirs(t):
    return t.reshape((-1,) + t.shape[2:])


def centred_shift(y, mu):
    prev = jnp.pad(y[:, :-1], ((0, 0), (1, 0), (0, 0)))
    nxt = jnp.pad(y[:, 1:], ((0, 0), (0, 1), (0, 0)))
    return y + mu[0] * (prev - y) + mu[1] * (nxt - y)


def centred_dwconv(x, w, b):
    K, C = w.shape
    out = lax.conv_general_dilated(
        x, w[:, None, :], window_strides=(1,), padding=[((K - 1) // 2, (K - 1) // 2)],
        dimension_numbers=("NWC", "WIO", "NWC"), feature_group_count=C)
    return out + b


def rwkv7_scan(r, w, k, v, a, b):
    n, H, K = r.shape[1], r.shape[2], r.shape[3]

    def step(state, inp):
        r_t, w_t, k_t, v_t, a_t, b_t = inp
        sa = jnp.einsum("nhvk,nhk->nhv", state, a_t)
        state = (state * w_t[:, :, None, :] + sa[..., None] * b_t[:, :, None, :]
                 + v_t[..., None] * k_t[:, :, None, :])
        return state, jnp.einsum("nhvk,nhk->nhv", state, r_t)

    _, y = lax.scan(step, jnp.zeros((n, H, K, K), r.dtype), (r, w, k, v, a, b))
    return y


def rwkv7_mixer(cols, mu, w0, w2, a0, a2, g2, k_k, k_a, r_k, ln_w):
    bsz, seq, _ = cols.shape
    cols = centred_shift(cols, mu).astype(jnp.float32)
    r, k, v, w_lr, a_lr, g_lr = _split(
        cols, [RW_DIM, RW_DIM, RW_DIM, RW_DECAY_RANK, RW_AAA_RANK, RW_GATE_RANK])
    heads = lambda t: t.reshape(t.shape[:-1] + (RW_HEADS, RW_HEAD))
    w_log = -jax.nn.softplus(-(w0[:, None, None, :]
                               + jnp.einsum("bsr,drc->dbsc", jnp.tanh(w_lr), w2))) - 0.5
    decay = jnp.exp(-jnp.exp(w_log))
    a = jax.nn.sigmoid(a0 + a_lr @ a2)
    g = jax.nn.sigmoid(g_lr) @ g2
    kk = heads(k * k_k)
    kk = kk * lax.rsqrt(jnp.maximum(jnp.sum(kk * kk, axis=-1, keepdims=True), 1e-12))
    k = k * (1.0 + (a - 1.0) * k_a)
    r_h, k_h, v_h, a_h = heads(r), heads(k), heads(v), heads(a)
    tm = lambda t: jnp.swapaxes(merge_dirs(t), 0, 1)
    y = rwkv7_scan(tm(to_dirs(r_h)), tm(flip_dir1(heads(decay))), tm(to_dirs(k_h)),
                   tm(to_dirs(v_h)), tm(to_dirs(-kk)), tm(to_dirs(kk * a_h)))
    y = flip_dir1(jnp.swapaxes(y, 0, 1).reshape(2, bsz, seq, RW_HEADS, RW_HEAD)).sum(0)
    y = head_norm(y, ln_w, RW_GN_EPS)
    bonus = (jnp.sum(r_h * k_h * r_k, axis=-1, keepdims=True) * v_h).reshape(bsz, seq, RW_DIM)
    return (y + bonus) * g


def segsum_exp(a):
    T = a.shape[-1]
    cs = jnp.cumsum(a, axis=-1)
    tri = jnp.tril(jnp.ones((T, T), bool))
    return jnp.exp(jnp.where(tri, cs[..., :, None] - cs[..., None, :], -jnp.inf))


def ssd_chunked(xdt, la, bm, cm, chunk):
    n, S, G, E, P = xdt.shape
    N = bm.shape[-1]
    nc = S // chunk
    xdt = xdt.reshape(n, nc, chunk, G, E, P)
    bm = bm.reshape(n, nc, chunk, G, N)
    cm = cm.reshape(n, nc, chunk, G, N)
    la = jnp.moveaxis(la.reshape(n, nc, chunk, G, E), (3, 4), (1, 2))
    a_cs = jnp.cumsum(la, axis=-1)
    scores = jnp.einsum("nclgd,ncsgd->ngcls", cm, bm)
    m = scores[:, :, None] * segsum_exp(la)
    y_diag = jnp.einsum("ngecls,ncsgep->nclgep", m, xdt)
    decay_states = jnp.exp(a_cs[..., -1:] - a_cs)
    states = jnp.einsum("nclgd,ngecl,nclgep->ncgepd", bm, decay_states, xdt)
    states = jnp.concatenate([jnp.zeros_like(states[:, :1]), states], axis=1)
    chunk_decay = segsum_exp(jnp.pad(a_cs[..., -1], ((0, 0), (0, 0), (0, 0), (1, 0))))
    states = jnp.einsum("ngezc,ncgepd->nzgepd", chunk_decay, states)[:, :-1]
    y_off = jnp.einsum("nclgd,ncgepd,ngecl->nclgep", cm, states, jnp.exp(a_cs))
    return (y_diag + y_off).reshape(n, S, G, E, P)


def mamba2_mixer(cols, conv_w, conv_b, dt_bias, A_log, Dskip, norm_w):
    bsz, seq, _ = cols.shape
    f32 = jnp.float32
    z, xbc, dt = _split(cols, [MB_DIM, MB_XBC, 2 * MB_HEADS])
    xbc = jax.nn.silu(centred_dwconv(xbc, conv_w, conv_b))
    xs, bm, cm = _split(xbc, [MB_DIM, MB_GROUPS * MB_STATE, MB_GROUPS * MB_STATE])
    xs = xs.reshape(bsz, seq, MB_GROUPS, MB_HPG, MB_HEADDIM).astype(f32)
    bm = bm.reshape(bsz, seq, MB_GROUPS, MB_STATE).astype(f32)
    cm = cm.reshape(bsz, seq, MB_GROUPS, MB_STATE).astype(f32)
    dt = jax.nn.softplus(dt.reshape(bsz, seq, 2, MB_HEADS).astype(f32) + dt_bias.astype(f32))
    dt = flip_dir1(jnp.moveaxis(dt, 2, 0)).reshape(2, bsz, seq, MB_GROUPS, MB_HPG)
    A = -jnp.exp(A_log.astype(f32)).reshape(2, 1, 1, MB_GROUPS, MB_HPG)
    xdt = to_dirs(xs) * dt[..., None]
    y = ssd_chunked(merge_dirs(xdt), merge_dirs(dt * A), merge_dirs(to_dirs(bm)),
                    merge_dirs(to_dirs(cm)), MB_CHUNK)
    y = flip_dir1(y.reshape((2, bsz) + y.shape[1:])).sum(0)
    y = y + Dskip.reshape(MB_GROUPS, MB_HPG, 1) * xs
    y = (y.reshape(bsz, seq, MB_DIM) * jax.nn.silu(z.astype(f32)))
    y = y.reshape(bsz, seq, MB_GROUPS, MB_DIM // MB_GROUPS)
    y = y * lax.rsqrt(jnp.mean(y * y, axis=-1, keepdims=True) + EPS)
    return y.reshape(bsz, seq, MB_DIM) * norm_w


def _s5_combine(e1, e2):
    a1r, a1i, b1r, b1i = e1
    a2r, a2i, b2r, b2i = e2
    return (a2r * a1r - a2i * a1i, a2r * a1i + a2i * a1r,
            a2r * b1r - a2i * b1i + b2r, a2r * b1i + a2i * b1r + b2i)


def s5_mixer(u, A_re, A_im, log_dt, B_re, B_im, C_re, C_im, Dskip, glu_w, glu_b):
    bsz, seq, _ = u.shape
    f32 = jnp.float32
    uf = u.astype(f32)
    ug = uf.reshape(bsz, seq, S5_GROUPS, S5_GROUP)
    Bre, Bim = B_re.astype(f32), B_im.astype(f32)
    y = Dskip * uf
    for d in range(2):
        dt = jnp.exp(log_dt[d].astype(f32))[:, None]
        ar = jnp.minimum(A_re[d].astype(f32), -1e-4)
        ai = A_im[d].astype(f32)
        mag = jnp.exp(dt * ar)
        abr, abi = mag * jnp.cos(dt * ai), mag * jnp.sin(dt * ai)
        den = ar * ar + ai * ai
        fr = ((abr - 1.0) * ar + abi * ai) / den
        fi = (abi * ar - (abr - 1.0) * ai) / den
        bbr = fr[..., None] * Bre - fi[..., None] * Bim
        bbi = fr[..., None] * Bim + fi[..., None] * Bre
        ud = ug if d == 0 else jnp.flip(ug, axis=1)
        bur = jnp.einsum("bsgm,gpm->bsgp", ud, bbr)
        bui = jnp.einsum("bsgm,gpm->bsgp", ud, bbi)
        a_shape = (1, seq, S5_GROUPS, S5_STATE)
        _, _, xr, xi = lax.associative_scan(
            _s5_combine, (jnp.broadcast_to(abr, a_shape), jnp.broadcast_to(abi, a_shape), bur, bui),
            axis=1)
        yd = (jnp.einsum("bsgp,gmp->bsgm", xr, C_re[d].astype(f32))
              - jnp.einsum("bsgp,gmp->bsgm", xi, C_im[d].astype(f32)))
        if d == 1:
            yd = jnp.flip(yd, axis=1)
        y = y + yd.reshape(bsz, seq, S5_DIM)
    y = jax.nn.gelu(y)
    return y * jax.nn.sigmoid(y @ glu_w + glu_b)


def mlstm_chunkwise(q, k, v, log_i, log_f, chunk):
    n, S, H, dh = q.shape
    nc = S // chunk

    def to_chunks(t):
        t = t.reshape((n, nc, chunk, H) + t.shape[3:])
        return jnp.moveaxis(t, (1, 3), (0, 2))

    tri = jnp.tril(jnp.ones((chunk, chunk), bool))

    def step(carry, inp):
        C, nv, m = carry
        qc, kc, vc, li, lf = inp
        b = jnp.cumsum(lf, axis=-1)
        log_d = jnp.where(tri, b[..., :, None] - b[..., None, :] + li[..., None, :], -jnp.inf)
        inter = b + m[..., None]
        m_t = jnp.maximum(jnp.max(log_d, axis=-1), inter)
        s = jnp.einsum("nhtd,nhsd->nhts", qc, kc) * jnp.exp(log_d - m_t[..., None])
        w_in = jnp.exp(inter - m_t)
        num = (jnp.einsum("nhts,nhsv->nhtv", s, vc)
               + w_in[..., None] * jnp.einsum("nhtd,nhdv->nhtv", qc, C))
        den = s.sum(-1) + w_in * jnp.einsum("nhtd,nhd->nht", qc, nv)
        h = num / jnp.maximum(jnp.abs(den), jnp.exp(-m_t))[..., None]
        b_last = b[..., -1]
        log_w = b_last[..., None] - b + li
        m_new = jnp.maximum(b_last + m, jnp.max(log_w, axis=-1))
        w = jnp.exp(log_w - m_new[..., None])
        dec = jnp.exp(b_last + m - m_new)
        C = dec[..., None, None] * C + jnp.einsum("nhs,nhsd,nhsv->nhdv", w, kc, vc)
        nv = dec[..., None] * nv + jnp.einsum("nhs,nhsd->nhd", w, kc)
        return (C, nv, m_new), h

    init = (jnp.zeros((n, H, dh, dh), q.dtype), jnp.zeros((n, H, dh), q.dtype),
            jnp.zeros((n, H), q.dtype))
    _, h = lax.scan(step, init, (to_chunks(q), to_chunks(k), to_chunks(v),
                                 to_chunks(log_i), to_chunks(log_f)))
    return jnp.moveaxis(h, (0, 2), (1, 3)).reshape(n, S, H, dh)


def mlstm_mixer(cols, conv_w, conv_b, wq, wk, wv, i_b, f_b, norm_w, skip):
    bsz, seq, _ = cols.shape
    f32 = jnp.float32
    xm, o_pre, i_pre, f_pre = _split(cols, [ML_DIM, ML_DIM, 2 * ML_HEADS, 2 * ML_HEADS])
    xc = jax.nn.silu(centred_dwconv(xm, conv_w, conv_b))

    def headwise(t, w):
        t = t.reshape(bsz, seq, ML_DIM // ML_BLOCK, ML_BLOCK)
        return jnp.einsum("bsjc,jcd->bsjd", t, w).reshape(bsz, seq, ML_HEADS, ML_HEAD).astype(f32)

    q = headwise(xc, wq)
    k = headwise(xc, wk) * (ML_HEAD ** -0.5)
    v = headwise(xm, wv)

    def gate_dirs(pre, bias):
        g = pre.reshape(bsz, seq, 2, ML_HEADS).astype(f32) + bias.astype(f32)
        return merge_dirs(flip_dir1(jnp.moveaxis(g, 2, 0)))

    log_i = gate_dirs(i_pre, i_b)
    log_f = jax.nn.log_sigmoid(gate_dirs(f_pre, f_b))
    h = mlstm_chunkwise(merge_dirs(to_dirs(q)), merge_dirs(to_dirs(k)),
                        merge_dirs(to_dirs(v)), log_i, log_f, ML_CHUNK)
    h = flip_dir1(h.reshape(2, bsz, seq, ML_HEADS, ML_HEAD)).sum(0)
    h = head_norm(h, norm_w, EPS)
    return jax.nn.sigmoid(o_pre.astype(f32)) * h + skip * xc


def sq_relu_mlp(x, w1, w2):
    return jnp.square(jax.nn.relu(x @ w1)) @ w2


def setup_inputs(seed: int = 0) -> dict:
    key = jax.random.key(seed)
    ks = iter(jax.random.split(key, 64))
    f32 = jnp.float32
    nrm = lambda shape, scale: scale * jax.random.normal(next(ks), shape, f32)
    uni = lambda shape, lo, hi: jax.random.uniform(next(ks), shape, f32, lo, hi)
    gain = lambda shape: 1.0 + nrm(shape, 0.02)

    x = nrm((BATCH, SEQ, D_MODEL), 1.0)
    norm_mix = gain((DEPTH, D_MODEL))
    norm_mlp = gain((DEPTH, D_MODEL))
    norm_final = gain((D_MODEL,))
    mlp_w1 = nrm((DEPTH, D_MODEL, D_FF), D_MODEL ** -0.5)
    mlp_w2 = nrm((DEPTH, D_FF, D_MODEL), 0.5 * D_FF ** -0.5)

    ab_w_in = nrm((N_AB, D_MODEL, AB_IN), D_MODEL ** -0.5)
    ab_w_out = nrm((N_AB, AB_OUT, D_MODEL), AB_OUT ** -0.5)
    rw_mu = uni((N_AB, 2, RW_COLS), 0.0, 0.4)
    rw_w0 = uni((N_AB, 2, RW_DIM), -6.0, 1.0)
    rw_w2 = nrm((N_AB, 2, RW_DECAY_RANK, RW_DIM), 0.1)
    rw_a0 = nrm((N_AB, RW_DIM), 0.1)
    rw_a2 = nrm((N_AB, RW_AAA_RANK, RW_DIM), 0.1)
    rw_g2 = nrm((N_AB, RW_GATE_RANK, RW_DIM), RW_GATE_RANK ** -0.5)
    rw_k_k = 0.85 + nrm((N_AB, RW_DIM), 0.02)
    rw_k_a = gain((N_AB, RW_DIM))
    rw_r_k = nrm((N_AB, RW_HEADS, RW_HEAD), 0.1)
    rw_ln_w = gain((N_AB, RW_DIM))
    mb_conv_w = nrm((N_AB, SHORT_CONV, MB_XBC), SHORT_CONV ** -0.5)
    mb_conv_b = nrm((N_AB, MB_XBC), 0.02)
    dt0 = jnp.exp(uni((N_AB, 2, MB_HEADS), float(np.log(1e-3)), float(np.log(1e-1))))
    mb_dt_bias = dt0 + jnp.log(-jnp.expm1(-dt0))
    mb_A_log = jnp.log(uni((N_AB, 2, MB_HEADS), 1.0, 16.0))
    mb_D = gain((N_AB, MB_HEADS))
    mb_norm_w = gain((N_AB, MB_DIM))

    cd_w_in = nrm((N_CD, D_MODEL, CD_IN), D_MODEL ** -0.5)
    cd_w_out = nrm((N_CD, CD_OUT, D_MODEL), CD_OUT ** -0.5)
    s5_A_re = -0.5 + nrm((N_CD, 2, S5_GROUPS, S5_STATE), 0.01)
    s5_A_im = (jnp.pi * jnp.arange(S5_STATE, dtype=f32)
               + nrm((N_CD, 2, S5_GROUPS, S5_STATE), 0.01))
    s5_log_dt = uni((N_CD, 2, S5_GROUPS), float(np.log(1e-3)), float(np.log(1e-1)))
    s5_B_re = nrm((N_CD, S5_GROUPS, S5_STATE, S5_GROUP), (2 * S5_GROUP) ** -0.5)
    s5_B_im = nrm((N_CD, S5_GROUPS, S5_STATE, S5_GROUP), (2 * S5_GROUP) ** -0.5)
    s5_C_re = nrm((N_CD, 2, S5_GROUPS, S5_GROUP, S5_STATE), S5_STATE ** -0.5)
    s5_C_im = nrm((N_CD, 2, S5_GROUPS, S5_GROUP, S5_STATE), S5_STATE ** -0.5)
    s5_D = nrm((N_CD, S5_DIM), 1.0)
    s5_glu_w = nrm((N_CD, S5_DIM, S5_DIM), S5_DIM ** -0.5)
    s5_glu_b = nrm((N_CD, S5_DIM), 0.02)
    ml_conv_w = nrm((N_CD, SHORT_CONV, ML_DIM), SHORT_CONV ** -0.5)
    ml_conv_b = nrm((N_CD, ML_DIM), 0.02)
    ml_wq = nrm((N_CD, ML_DIM // ML_BLOCK, ML_BLOCK, ML_BLOCK), ML_BLOCK ** -0.5)
    ml_wk = nrm((N_CD, ML_DIM // ML_BLOCK, ML_BLOCK, ML_BLOCK), ML_BLOCK ** -0.5)
    ml_wv = nrm((N_CD, ML_DIM // ML_BLOCK, ML_BLOCK, ML_BLOCK), ML_BLOCK ** -0.5)
    ml_i_b = nrm((N_CD, 2, ML_HEADS), 0.1)
    ml_f_b = jnp.linspace(3.0, 6.0, ML_HEADS, dtype=f32) + nrm((N_CD, 2, ML_HEADS), 0.02)
    ml_norm_w = gain((N_CD, ML_DIM))
    ml_skip = gain((N_CD, ML_DIM))

    return {
        "x": x, "norm_mix": norm_mix, "norm_mlp": norm_mlp, "norm_final": norm_final,
        "mlp_w1": mlp_w1, "mlp_w2": mlp_w2,
        "ab_w_in": ab_w_in, "ab_w_out": ab_w_out,
        "rw_mu": rw_mu, "rw_w0": rw_w0, "rw_w2": rw_w2, "rw_a0": rw_a0, "rw_a2": rw_a2,
        "rw_g2": rw_g2, "rw_k_k": rw_k_k, "rw_k_a": rw_k_a, "rw_r_k": rw_r_k,
        "rw_ln_w": rw_ln_w,
        "mb_conv_w": mb_conv_w, "mb_conv_b": mb_conv_b, "mb_dt_bias": mb_dt_bias,
        "mb_A_log": mb_A_log, "mb_D": mb_D, "mb_norm_w": mb_norm_w,
        "cd_w_in": cd_w_in, "cd_w_out": cd_w_out,
        "s5_A_re": s5_A_re, "s5_A_im": s5_A_im, "s5_log_dt": s5_log_dt,
        "s5_B_re": s5_B_re, "s5_B_im": s5_B_im, "s5_C_re": s5_C_re, "s5_C_im": s5_C_im,
        "s5_D": s5_D, "s5_glu_w": s5_glu_w, "s5_glu_b": s5_glu_b,
        "ml_conv_w": ml_conv_w, "ml_conv_b": ml_conv_b, "ml_wq": ml_wq, "ml_wk": ml_wk,
        "ml_wv": ml_wv, "ml_i_b": ml_i_b, "ml_f_b": ml_f_b, "ml_norm_w": ml_norm_w,
        "ml_skip": ml_skip,
    }


def reference(x, norm_mix, norm_mlp, norm_final, mlp_w1, mlp_w2,
              ab_w_in, ab_w_out, rw_mu, rw_w0, rw_w2, rw_a0, rw_a2, rw_g2, rw_k_k, rw_k_a,
              rw_r_k, rw_ln_w, mb_conv_w, mb_conv_b, mb_dt_bias, mb_A_log, mb_D, mb_norm_w,
              cd_w_in, cd_w_out, s5_A_re, s5_A_im, s5_log_dt, s5_B_re, s5_B_im, s5_C_re,
              s5_C_im, s5_D, s5_glu_w, s5_glu_b, ml_conv_w, ml_conv_b, ml_wq, ml_wk, ml_wv,
              ml_i_b, ml_f_b, ml_norm_w, ml_skip):
    h = x
    for layer in range(DEPTH):
        xn = rmsnorm(h, norm_mix[layer])
        i = layer // 2
        if layer % 2 == 0:
            cols = xn @ ab_w_in[i]
            rw_cols, mb_cols = _split(cols, [RW_COLS, MB_COLS])
            y_rw = rwkv7_mixer(rw_cols, rw_mu[i], rw_w0[i], rw_w2[i], rw_a0[i], rw_a2[i],
                               rw_g2[i], rw_k_k[i], rw_k_a[i], rw_r_k[i], rw_ln_w[i])
            y_mb = mamba2_mixer(mb_cols, mb_conv_w[i], mb_conv_b[i], mb_dt_bias[i],
                                mb_A_log[i], mb_D[i], mb_norm_w[i])
            mixed = jnp.concatenate([y_rw, y_mb], axis=-1).astype(xn.dtype) @ ab_w_out[i]
        else:
            cols = xn @ cd_w_in[i]
            s5_cols, ml_cols = _split(cols, [S5_DIM, ML_COLS])
            y_s5 = s5_mixer(s5_cols, s5_A_re[i], s5_A_im[i], s5_log_dt[i], s5_B_re[i],
                            s5_B_im[i], s5_C_re[i], s5_C_im[i], s5_D[i], s5_glu_w[i], s5_glu_b[i])
            y_ml = mlstm_mixer(ml_cols, ml_conv_w[i], ml_conv_b[i], ml_wq[i], ml_wk[i], ml_wv[i],
                               ml_i_b[i], ml_f_b[i], ml_norm_w[i], ml_skip[i])
            mixed = jnp.concatenate([y_s5, y_ml], axis=-1).astype(xn.dtype) @ cd_w_out[i]
        h = h + mixed
        h = h + sq_relu_mlp(rmsnorm(h, norm_mlp[layer]), mlp_w1[layer], mlp_w2[layer])
    return rmsnorm(h, norm_final)
```

```python
import math
import numpy as np
from contextlib import ExitStack, contextmanager
import concourse.bass as bass
import concourse.mybir as mybir
from concourse.bass_utils import run_bass_kernel_spmd

F32 = mybir.dt.float32
BF16 = mybir.dt.bfloat16
AF = mybir.ActivationFunctionType
ALU = mybir.AluOpType
AX = mybir.AxisListType

NDS = 32
import os as _os0
SAME_ENG_WINDOW = int(_os0.environ.get("SEW", "6"))
MMDT = BF16
import os as _os
USE_POOL = False


class V:
    __slots__ = ("t", "ap")

    def __init__(self, t, ap):
        self.t = t
        self.ap = ap

    def __getitem__(self, idx):
        return V(self.t, self.ap[idx])

    def r(self, pat, **kw):
        return V(self.t, self.ap.rearrange(pat, **kw))

    def bc(self, shape):
        return V(self.t, self.ap.to_broadcast(list(shape)))

    def us(self, axis):
        return V(self.t, self.ap.unsqueeze(axis))

    def pb(self, n):
        return V(self.t, self.ap.partition_broadcast(n))


class Tl:
    __slots__ = ("h", "lw", "rd", "excl", "acc", "name")

    def __init__(self, h, name="", excl=False, acc=False):
        self.h = h
        self.lw = {}
        self.rd = {}
        self.excl = excl
        self.acc = acc
        self.name = name

    def __getitem__(self, idx):
        return V(self, self.h[idx])

    def v(self):
        return V(self, self.h)


class Ring:
    def __init__(self, tiles):
        self.tiles = tiles
        self.i = 0

    def next(self):
        t = self.tiles[self.i % len(self.tiles)]
        self.i += 1
        return t


class K:
    def __init__(self, nc, es):
        self.nc = nc
        self.stack = [es]
        self.E = dict(pe=nc.tensor, dve=nc.vector, act=nc.scalar, pool=nc.gpsimd, sp=nc.sync)
        self.sem = {e: es.enter_context(nc.semaphore("s_" + e)) for e in self.E}
        self.cnt = {e: 0 for e in self.E}
        self.waited = {}
        self.dsem = [es.enter_context(nc.semaphore("d%d" % i)) for i in range(NDS)]
        self.dcnt = [0] * NDS
        self.dnext = 0
        self.dnext_sw = 0
        self.nid = 0
        self.out_tickets = []
        self.PS = [Tl(es.enter_context(nc.psum_tensor("psb%d" % i, [128, 512], F32)), "psb%d" % i, excl=True)
                   for i in range(8)]
        self.psi = 0
        self.evi = 0

    @contextmanager
    def scope(self):
        es = ExitStack()
        self.stack.append(es)
        try:
            yield
        finally:
            self.barrier()
            self.stack.pop()
            es.close()

    def sb(self, shape, dt=F32, name=None):
        self.nid += 1
        name = (name or "t") + "_%d" % self.nid
        return Tl(self.stack[-1].enter_context(self.nc.sbuf_tensor(name, list(shape), dt)), name)

    def ring(self, n, shape, dt=F32, name=None):
        return Ring([self.sb(shape, dt, name) for _ in range(n)])

    def psn(self):
        t = self.PS[self.psi % 8]
        self.psi += 1
        return t

    def dram(self, name, shape, dt=F32):
        return Tl(self.nc.dram_tensor(name, list(shape), dt, kind="Internal").ap(), name, acc=True)

    def _wait(self, e, tk):
        sem, val, key, teng = tk
        if self.waited.get((e, key), 0) >= val:
            return
        self.E[e].wait_ge(sem, val)
        self.waited[(e, key)] = val

    def _need(self, e, tk):
        if tk[3] != e:
            return True
        if e == "pe":
            return False
        return self.cnt[e] - tk[1] < SAME_ENG_WINDOW

    def _deps(self, e, outs, ins):
        for t in ins:
            if t.excl:
                continue
            for tk in t.lw.values():
                if self._need(e, tk):
                    self._wait(e, tk)
        for t in list(outs) + [t for t in ins if t.excl]:
            if not t.acc:
                for tk in t.lw.values():
                    if self._need(e, tk):
                        self._wait(e, tk)
            for rk in t.rd.values():
                if self._need(e, rk):
                    self._wait(e, rk)

    def _mark(self, tk, outs, ins):
        for t in ins:
            if t.excl:
                continue
            t.rd[tk[2]] = tk
        for t in list(outs) + [t for t in ins if t.excl]:
            if t.acc:
                t.lw[tk[2]] = tk
            else:
                t.lw = {tk[2]: tk}
                t.rd = {}

    def op(self, e, fn, outs=(), ins=()):
        self._deps(e, outs, ins)
        ins_obj = fn(self.E[e])
        self.cnt[e] += 1
        ins_obj.then_inc(self.sem[e], 1)
        tk = (self.sem[e], self.cnt[e], e, e)
        self._mark(tk, outs, ins)
        return tk

    def dma(self, q, out, in_, final=False):
        outs = [out.t]
        ins = [in_.t]
        self._deps(q, outs, ins)
        if q == "pool":
            i = NDS - 8 + self.dnext_sw
            self.dnext_sw = (self.dnext_sw + 1) % 8
        else:
            i = self.dnext
            self.dnext = (self.dnext + 1) % (NDS - 8)
        if self.dcnt[i] > 0:
            self._wait(q, (self.dsem[i], 16 * self.dcnt[i], ("d", i), "dma"))
        ins_obj = self.E[q].dma_start(out=out.ap, in_=in_.ap)
        self.dcnt[i] += 1
        ins_obj.then_inc(self.dsem[i], 16)
        tk = (self.dsem[i], 16 * self.dcnt[i], ("d", i), "dma")
        self._mark(tk, outs, ins)
        if final:
            self.out_tickets.append(tk)
        return tk

    def barrier(self):
        for e in self.E:
            for x in self.E:
                if x != e and self.cnt[x] > 0:
                    self._wait(e, (self.sem[x], self.cnt[x], x, x))
            for i in range(NDS):
                if self.dcnt[i] > 0:
                    self._wait(e, (self.dsem[i], 16 * self.dcnt[i], ("d", i), "dma"))

    def finish(self):
        self.barrier()

    @staticmethod
    def _sv(x, ins):
        if isinstance(x, V):
            ins.append(x.t)
            return x.ap
        return x

    def act(self, out, in_, func, bias=None, scale=1.0):
        ins = [in_.t]
        kw = dict(out=out.ap, in_=in_.ap, func=func)
        if bias is not None:
            kw["bias"] = self._sv(bias, ins)
        kw["scale"] = self._sv(scale, ins)
        return self.op("act", lambda e: e.activation(**kw), outs=[out.t], ins=ins)

    def tt(self, out, a, b, op, eng="dve"):
        return self.op(eng, lambda e: e.tensor_tensor(out=out.ap, in0=a.ap, in1=b.ap, op=op),
                       outs=[out.t], ins=[a.t, b.t])

    def ts(self, out, a, s1, op0, s2=None, op1=None, eng="dve"):
        ins = [a.t]
        kw = dict(out=out.ap, in0=a.ap, scalar1=self._sv(s1, ins), scalar2=self._sv(s2, ins), op0=op0)
        if op1 is not None:
            kw["op1"] = op1
        return self.op(eng, lambda e: e.tensor_scalar(**kw), outs=[out.t], ins=ins)

    def stt(self, out, a, s, b, op0, op1, eng="dve"):
        ins = [a.t, b.t]
        sc = self._sv(s, ins)
        return self.op(eng, lambda e: e.scalar_tensor_tensor(out=out.ap, in0=a.ap, scalar=sc, in1=b.ap,
                                                              op0=op0, op1=op1), outs=[out.t], ins=ins)

    def copy(self, out, in_, eng="dve"):
        if eng == "act":
            return self.op("act", lambda e: e.activation(out=out.ap, in_=in_.ap, func=AF.Copy),
                           outs=[out.t], ins=[in_.t])
        return self.op(eng, lambda e: e.tensor_copy(out=out.ap, in_=in_.ap), outs=[out.t], ins=[in_.t])

    def evac(self, out, in_):
        self.evi += 1
        return self.copy(out, in_, "act" if self.evi % 2 else "dve")

    def recip(self, out, in_):
        return self.op("dve", lambda e: e.reciprocal(out=out.ap, in_=in_.ap), outs=[out.t], ins=[in_.t])

    def memset(self, out, val, eng="pool"):
        return self.op(eng, lambda e: e.memset(out.ap, val), outs=[out.t])

    def mm(self, out, lhsT, rhs, start=True, stop=True):
        return self.op("pe", lambda e: e.matmul(out.ap, lhsT=lhsT.ap, rhs=rhs.ap, start=start, stop=stop),
                       outs=[out.t], ins=[lhsT.t, rhs.t])

    def tr(self, out, in_, ident):
        return self.op("pe", lambda e: e.transpose(out.ap, in_.ap, ident.ap), outs=[out.t], ins=[in_.t, ident.t])

    def scan(self, out, d0, d1, init=0.0, op0=ALU.mult, op1=ALU.add):
        ins = [d0.t, d1.t]
        iv = self._sv(init, ins)
        return self.op("dve", lambda e: e.tensor_tensor_scan(out=out.ap, data0=d0.ap, data1=d1.ap, initial=iv,
                                                              op0=op0, op1=op1), outs=[out.t], ins=ins)

    def reduce(self, out, in_, op=ALU.add, axis=AX.X):
        return self.op("dve", lambda e: e.tensor_reduce(out=out.ap, in_=in_.ap, op=op, axis=axis),
                       outs=[out.t], ins=[in_.t])

    def aselect(self, out, in_, pattern, cmp, fill, base, cm):
        return self.op("pool", lambda e: e.affine_select(out=out.ap, in_=in_.ap, pattern=pattern, compare_op=cmp,
                                                         fill=fill, base=base, channel_multiplier=cm),
                       outs=[out.t], ins=[in_.t])


D = 1024
DFF = 4096
EPS = 1e-5
RW_GN_EPS = 64e-5
AB_IN = 4384
CD_IN = 2592

IN_SPECS = [
    ("norm_mix", (2, 1024)), ("norm_mlp", (2, 1024)), ("norm_final", (1024,)),
    ("mlp_w1", (2, 1024, 4096)), ("mlp_w2", (2, 4096, 1024)),
    ("ab_w_in", (1, 1024, 4384)), ("ab_w_out", (1, 1536, 1024)),
    ("rw_mu", (1, 2, 1792)), ("rw_w0", (1, 2, 512)), ("rw_w2", (1, 2, 64, 512)), ("rw_a0", (1, 512)),
    ("rw_a2", (1, 64, 512)), ("rw_g2", (1, 128, 512)), ("rw_k_k", (1, 512)), ("rw_k_a", (1, 512)),
    ("rw_r_k", (1, 8, 64)), ("rw_ln_w", (1, 512)),
    ("mb_conv_w", (1, 3, 1536)), ("mb_conv_b", (1, 1536)), ("mb_dt_bias", (1, 2, 16)),
    ("mb_A_log", (1, 2, 16)), ("mb_D", (1, 16)), ("mb_norm_w", (1, 1024)),
    ("cd_w_in", (1, 1024, 2592)), ("cd_w_out", (1, 1536, 1024)),
    ("s5_A_re", (1, 2, 32, 64)), ("s5_A_im", (1, 2, 32, 64)), ("s5_log_dt", (1, 2, 32)),
    ("s5_B_re", (1, 32, 64, 16)), ("s5_B_im", (1, 32, 64, 16)),
    ("s5_C_re", (1, 2, 32, 16, 64)), ("s5_C_im", (1, 2, 32, 16, 64)),
    ("s5_D", (1, 512)), ("s5_glu_w", (1, 512, 512)), ("s5_glu_b", (1, 512)),
    ("ml_conv_w", (1, 3, 1024)), ("ml_conv_b", (1, 1024)),
    ("ml_wq", (1, 256, 4, 4)), ("ml_wk", (1, 256, 4, 4)), ("ml_wv", (1, 256, 4, 4)),
    ("ml_i_b", (1, 2, 8)), ("ml_f_b", (1, 2, 8)), ("ml_norm_w", (1, 1024)), ("ml_skip", (1, 1024)),
]


class Ctx:
    pass


def col(ap1d, p=128):
    return ap1d.rearrange("(c p) -> p c", p=p)


def stage_consts(k, C):
    C.ident = k.sb([128, 128], name="ident")
    k.memset(C.ident[:], 1.0)
    k.aselect(C.ident[:], C.ident[:], [[-1, 128]], ALU.is_equal, 0.0, 0, 1)
    C.ones = k.sb([128, 128], name="ones")
    k.memset(C.ones[:], 1.0)
    C.bo = k.sb([128, 128], name="blockones")
    k.memset(C.bo[:], 0.0)
    k.memset(C.bo[0:64, 0:64], 1.0)
    k.memset(C.bo[64:128, 64:128], 1.0)
    C.cst = k.sb([128, 8], name="cst")
    for i, v in enumerate([0.0, 1.0, -0.5, EPS, RW_GN_EPS]):
        k.memset(C.cst[:, i:i + 1], v)
    C.mlow = k.sb([128, 128], name="mlow")
    k.memset(C.mlow[:], 1.0)
    k.aselect(C.mlow[:], C.mlow[:], [[1, 128]], ALU.is_ge, 0.0, 0, -1)
    C.mup = k.sb([128, 128], name="mup")
    k.memset(C.mup[:], 1.0)
    k.aselect(C.mup[:], C.mup[:], [[-1, 128]], ALU.is_ge, 0.0, 0, 1)
    C.sel = k.sb([32, 32, 128], name="sel")
    k.memset(C.sel[:], 1.0)
    k.aselect(C.sel[:], C.sel[:], [[-1, 32], [0, 128]], ALU.is_equal, 0.0, 0, 1)
    C.hm = k.sb([128, 2], name="halfmask")
    k.memset(C.hm[:], 0.0)
    k.memset(C.hm[0:64, 0:1], 1.0)
    k.memset(C.hm[64:128, 1:2], 1.0)


def stage_load_x(k, C):
    T = C.T
    xf = C.x.r("b s d -> (b s) d")
    with k.scope():
        xr = k.ring(2, [128, 4, 1024], name="xin")
        hr = k.ring(3, [128, 512], name="hout")
        for tb in range(T // 512):
            xt = xr.next()
            k.dma("sp", xt[:], xf[tb * 512:(tb + 1) * 512, :].r("(j p) d -> p j d", p=128))
            for c in range(8):
                ps = k.psn()
                for j in range(4):
                    k.tr(ps[:, j * 128:(j + 1) * 128], xt[:, j, c * 128:(c + 1) * 128], C.ident[:])
                ho = hr.next()
                k.evac(ho[:], ps[:])
                k.dma("pool", C.hT[c * 128:(c + 1) * 128, tb * 512:(tb + 1) * 512], ho[:])


def rmsnorm_block(k, C, wcol, tok0, ntok, xn, R):
    for t0 in range(0, ntok, 512):
        h = R["h"].next()
        k.dma("sp", h[:], C.hT[:, tok0 + t0:tok0 + t0 + 512].r("(c p) t -> p c t", p=128))
        sq = R["sq"].next()
        k.act(sq[:], h[:], AF.Square)
        ps = k.psn()
        for c in range(8):
            k.mm(ps[:], C.ones[:], sq[:, c, :], start=(c == 0), stop=(c == 7))
        rs = R["rs"].next()
        k.act(rs[:], ps[:], AF.Sqrt, bias=C.cst[:, 3:4], scale=1.0 / D)
        k.recip(rs[:], rs[:])
        for c in range(8):
            k.stt(xn[:, c, t0:t0 + 512], h[:, c, :], wcol[:, c:c + 1], rs[:], ALU.mult, ALU.mult)


def norm_rings(k):
    return dict(h=k.ring(2, [128, 8, 512], name="nh"), sq=k.ring(1, [128, 8, 512], name="nsq"),
                rs=k.ring(2, [128, 512], name="nrs"))


def dense(k, xT, KC, W, N, ntok, epi, wring, wgrp):
    for n0 in range(0, N, wgrp):
        nw = min(wgrp, N - n0)
        wt = wring.next()
        k.dma("pool", wt[:, :, 0:nw], W[:, n0:n0 + nw].r("(c p) n -> p c n", p=128))
        for m0 in range(0, nw, 128):
            mw = min(128, nw - m0)
            for t0 in range(0, ntok, 512):
                ps = k.psn()
                for c in range(KC):
                    k.mm(ps[0:mw, :], wt[:, c, m0:m0 + mw], xT[:, c, t0:t0 + 512], start=(c == 0), stop=(c == KC - 1))
                epi(n0 + m0, mw, t0, ps)


def stage_inproj(k, C, layer, W, N, nw_ap):
    T, TB = C.T, C.TB
    with k.scope():
        R = norm_rings(k)
        wcol = k.sb([128, 8], name="nw")
        k.dma("sp", wcol[:], nw_ap)
        xn = k.sb([128, 8, TB], MMDT, name="xn")
        wring = k.ring(2, [128, 8, 512], MMDT, name="win")
        oring = k.ring(3, [128, 512], name="cout")
        import os
        DBG = os.environ.get("KDBG", "")
        for tb0 in range(0, T, TB):
            if "nonorm" in DBG:
                k.memset(xn[:], 1.0)
            else:
                rmsnorm_block(k, C, wcol, tb0, TB, xn, R)

            def epi(n, mw, t0, ps):
                o = oring.next()
                k.evac(o[0:mw, :], ps[0:mw, :])
                if "nostore" not in DBG:
                    k.dma("sp", C.colsT[n:n + mw, tb0 + t0:tb0 + t0 + 512], o[0:mw, :])
            if "nodense" not in DBG:
                dense(k, xn[:], 8, W, (512 if "small" in DBG else N), TB, epi, wring, 512)


def stage_outproj(k, C, W):
    T, TB = C.T, C.TB
    with k.scope():
        xin = k.sb([128, 12, TB], MMDT, name="mixin")
        wring = k.ring(2, [128, 12, 512], MMDT, name="wout")
        hring = k.ring(3, [128, 512], name="hres")
        for tb0 in range(0, T, TB):
            k.dma("pool", xin[:], C.mixT[:, tb0:tb0 + TB].r("(c p) t -> p c t", p=128))

            def epi(n, mw, t0, ps):
                hr = hring.next()
                k.dma("sp", hr[:], C.hT[n:n + 128, tb0 + t0:tb0 + t0 + 512])
                k.tt(hr[:], hr[:], ps[:], ALU.add)
                k.dma("sp", C.hT[n:n + 128, tb0 + t0:tb0 + t0 + 512], hr[:])
            dense(k, xin[:], 12, W, 1024, TB, epi, wring, 512)


def stage_mlp(k, C, layer):
    T, TB = C.T, C.TB
    W1 = C.inp["mlp_w1"][layer]
    W2 = C.inp["mlp_w2"][layer]
    with k.scope():
        R = norm_rings(k)
        wcol = k.sb([128, 8], name="nw")
        k.dma("sp", wcol[:], C.inp["norm_mlp"][layer].r("(c p) -> p c", p=128))
        xn = k.sb([128, 8, TB], MMDT, name="xn")
        hid = k.sb([128, 32, TB], MMDT, name="hid")
        w1r = k.ring(2, [128, 8, 512], MMDT, name="w1")
        w2r = k.ring(2, [128, 32, 128], MMDT, name="w2")
        tring = k.ring(2, [128, 512], name="relu")
        hring = k.ring(3, [128, 512], name="hres")
        for tb0 in range(0, T, TB):
            rmsnorm_block(k, C, wcol, tb0, TB, xn, R)

            def epi1(n, mw, t0, ps):
                tmp = tring.next()
                k.act(tmp[:], ps[:], AF.Relu)
                k.tt(hid[:, n // 128, t0:t0 + 512], tmp[:], tmp[:], ALU.mult)
            dense(k, xn[:], 8, W1, DFF, TB, epi1, w1r, 512)

            def epi2(n, mw, t0, ps):
                hr = hring.next()
                k.dma("sp", hr[:], C.hT[n:n + 128, tb0 + t0:tb0 + t0 + 512])
                k.tt(hr[:], hr[:], ps[:], ALU.add)
                k.dma("sp", C.hT[n:n + 128, tb0 + t0:tb0 + t0 + 512], hr[:])
            dense(k, hid[:], 32, W2, D, TB, epi2, w2r, 128)


def stage_final(k, C):
    T = C.T
    of = C.out.r("b s d -> (b s) d")
    with k.scope():
        R = norm_rings(k)
        wcol = k.sb([128, 8], name="nw")
        k.dma("sp", wcol[:], C.inp["norm_final"].v().r("(c p) -> p c", p=128))
        xnr = k.ring(2, [128, 8, 512], name="xnf")
        orr = k.ring(2, [128, 4, 1024], name="outt")
        for t0 in range(0, T, 512):
            xn = xnr.next()
            rmsnorm_block(k, C, wcol, t0, 512, xn, R)
            ot = orr.next()
            for j in range(4):
                for c0 in range(0, 8, 4):
                    ps = k.psn()
                    for c in range(c0, c0 + 4):
                        k.tr(ps[:, (c - c0) * 128:(c - c0 + 1) * 128], xn[:, c, j * 128:(j + 1) * 128], C.ident[:])
                    k.evac(ot[:, j, c0 * 128:(c0 + 4) * 128], ps[:])
            k.dma("sp", of[t0:t0 + 512, :].r("(j p) d -> p j d", p=128), ot[:], final=True)


def to_tokmajor(k, C, src_fn, nchunk, ntile, dst_fn, oring):
    for j in range(ntile):
        for c0 in range(0, nchunk, 4):
            n = min(4, nchunk - c0)
            ps = k.psn()
            for c in range(c0, c0 + n):
                k.tr(ps[:, (c - c0) * 128:(c - c0 + 1) * 128], src_fn(c, j), C.ident[:])
            o = oring.next()
            k.evac(o[:, 0:n * 128], ps[:, 0:n * 128])
            k.dma("sp", dst_fn(j, c0, n), o[:, 0:n * 128])


def softplus_neg(k, out, in_, nbias, tmp):
    k.act(tmp, in_, AF.Exp, bias=nbias, scale=-1.0)
    k.act(out, tmp, AF.Ln, bias=1.0)


def stage_rwkv_prep(k, C):
    S, NB, T = C.S, C.NB, C.T
    BLK = min(512, S)
    I = C.inp
    sc = C.scr
    with k.scope():
        mu = k.sb([128, 14, 2], name="mu")
        for m_ in range(2):
            k.dma("sp", mu[:, :, m_], I["rw_mu"][0, m_].r("(c p) -> p c", p=128))
        cmu = k.sb([128, 14], name="cmu")
        k.tt(cmu[:], mu[:, :, 0], mu[:, :, 1], ALU.add)
        k.ts(cmu[:], cmu[:], -1.0, ALU.mult, 1.0, ALU.add)
        w0n = k.sb([128, 2, 4], name="w0n")
        for d_ in range(2):
            k.dma("sp", w0n[:, d_, :], I["rw_w0"][0, d_].r("(c p) -> p c", p=128))
        k.ts(w0n[:], w0n[:], -1.0, ALU.mult)
        w2 = k.sb([64, 2, 512], name="w2")
        k.dma("sp", w2[:], I["rw_w2"][0].r("d r c -> r d c"))
        a2 = k.sb([128, 512], name="a2")
        k.dma("sp", a2[64:128, :], I["rw_a2"][0])
        g2 = k.sb([128, 512], name="g2")
        k.dma("sp", g2[:], I["rw_g2"][0])
        pc = k.sb([128, 6, 4], name="pcols")
        for i, nm in enumerate(["rw_a0", "rw_k_k", "rw_k_a"]):
            k.dma("sp", pc[:, i, :], I[nm][0].r("(c p) -> p c", p=128))
        k.ts(pc[:, 3, :], pc[:, 2, :], -1.0, ALU.mult, 1.0, ALU.add)
        k.dma("sp", pc[:, 4, :], I["rw_r_k"][0].r("h k -> (h k)").r("(c p) -> p c", p=128))

        halo = k.ring(1, [128, 14, BLK + 2], name="halo")
        SHr = k.ring(1, [128, 14, BLK], name="sh")
        tmpr = k.ring(3, [128, BLK], name="rtmp")
        twr = k.ring(1, [128, BLK], name="rtw")
        sgr = k.ring(1, [128, BLK], name="rsg")
        outr = k.ring(4, [128, BLK], name="rout")
        Ar = k.ring(1, [128, 4, BLK], name="A")
        vecr = k.ring(1, [128, 7, 4, BLK], name="vecs")
        tokr = k.ring(3, [128, 512], name="tokout")
        for b in range(NB):
            for s0 in range(0, S, BLK):
                g0 = b * S + s0
                hl = halo.next()
                lo = 1 if s0 == 0 else 0
                hi = 1 if s0 + BLK == S else 0
                if lo:
                    k.memset(hl[:, :, 0:1], 0.0)
                if hi:
                    k.memset(hl[:, :, BLK + 1:BLK + 2], 0.0)
                k.dma("sp", hl[:, :, lo:BLK + 2 - hi],
                      C.colsT[0:1792, g0 - 1 + lo:g0 + BLK + 1 - hi].r("(c p) t -> p c t", p=128))
                SH = SHr.next()
                for c in range(14):
                    k.ts(SH[:, c, :], hl[:, c, 1:BLK + 1], cmu[:, c:c + 1], ALU.mult)
                    k.stt(SH[:, c, :], hl[:, c, 0:BLK], mu[:, c, 0:1], SH[:, c, :], ALU.mult, ALU.add)
                    k.stt(SH[:, c, :], hl[:, c, 2:BLK + 2], mu[:, c, 1:2], SH[:, c, :], ALU.mult, ALU.add)
                VE = vecr.next()
                tw = twr.next()
                k.act(tw[0:64, :], SH[0:64, 12, :], AF.Tanh)
                for d in range(2):
                    for cc in range(4):
                        ps = k.psn()
                        k.mm(ps[:, 0:BLK], w2[0:64, d, cc * 128:(cc + 1) * 128], tw[0:64, :])
                        t1 = tmpr.next()
                        softplus_neg(k, t1[:], ps[:, 0:BLK], w0n[:, d, cc:cc + 1], t1[:])
                        k.act(t1[:], t1[:], AF.Exp, bias=C.cst[:, 2:3], scale=-1.0)
                        k.ts(VE[:, 5 + d, cc, :], t1[:], -1.0, ALU.mult)
                A = Ar.next()
                sg = sgr.next()
                k.act(sg[:], SH[:, 13, :], AF.Sigmoid)
                for cc in range(4):
                    ps = k.psn()
                    k.mm(ps[:, 0:BLK], a2[64:128, cc * 128:(cc + 1) * 128], SH[64:128, 12, :])
                    k.act(A[:, cc, :], ps[:, 0:BLK], AF.Sigmoid, bias=pc[:, 0, cc:cc + 1])
                    ps = k.psn()
                    k.mm(ps[:, 0:BLK], g2[:, cc * 128:(cc + 1) * 128], sg[:])
                    o = outr.next()
                    k.evac(o[:], ps[:, 0:BLK])
                    k.dma("sp", sc["gT"][cc * 128:(cc + 1) * 128, g0:g0 + BLK], o[:])
                    kk = tmpr.next()
                    k.ts(kk[:], SH[:, 4 + cc, :], pc[:, 1, cc:cc + 1], ALU.mult)
                    sq = tmpr.next()
                    k.tt(sq[:], kk[:], kk[:], ALU.mult)
                    ps = k.psn()
                    k.mm(ps[:, 0:BLK], C.bo[:], sq[:])
                    k.ts(sq[:], ps[:, 0:BLK], 1e-12, ALU.max)
                    k.act(sq[:], sq[:], AF.Sqrt)
                    k.recip(sq[:], sq[:])
                    k.tt(kk[:], kk[:], sq[:], ALU.mult)
                    k.ts(VE[:, 4, cc, :], kk[:], -1.0, ALU.mult)
                    k.tt(VE[:, 1, cc, :], kk[:], A[:, cc, :], ALU.mult)
                    t2 = tmpr.next()
                    k.ts(t2[:], A[:, cc, :], pc[:, 2, cc:cc + 1], ALU.mult, pc[:, 3, cc:cc + 1], ALU.add)
                    k.tt(VE[:, 2, cc, :], SH[:, 4 + cc, :], t2[:], ALU.mult)
                    k.copy(VE[:, 0, cc, :], SH[:, cc, :], "pool")
                    k.copy(VE[:, 3, cc, :], SH[:, 8 + cc, :], "pool")
                    k.stt(t2[:], SH[:, cc, :], pc[:, 4, cc:cc + 1], VE[:, 2, cc, :], ALU.mult, ALU.mult)
                    ps = k.psn()
                    k.mm(ps[:, 0:BLK], C.bo[:], t2[:])
                    o = outr.next()
                    k.tt(o[:], ps[:, 0:BLK], SH[:, 8 + cc, :], ALU.mult)
                    k.dma("sp", sc["bonusT"][cc * 128:(cc + 1) * 128, g0:g0 + BLK], o[:])
                for vi, nm in enumerate(["rTok", "bTok", "kTok", "vTok", "aTok", "lw0Tok", "lw1Tok"]):
                    to_tokmajor(k, C, lambda c, j, vi=vi: VE[:, vi, c, j * 128:(j + 1) * 128], 4, BLK // 128,
                                lambda j, c0, n, nm=nm: sc[nm][g0 + j * 128:g0 + (j + 1) * 128, :], tokr)


def stage_rwkv_scan(k, C):
    S, NB, T = C.S, C.NB, C.T
    L = 64
    NCK = S // L
    sc = C.scr
    H8 = 8
    with k.scope():
        def mk(name, pattern, cm, cmp):
            m = k.sb([L, L], name=name)
            k.memset(m[:], 1.0)
            k.aselect(m[:], m[:], pattern, cmp, 0.0, 0, cm)
            return m
        UP = mk("mUP", [[1, L]], -1, ALU.is_gt)
        LO = mk("mLO", [[-1, L]], 1, ALU.is_gt)
        UPI = mk("mUPI", [[1, L]], -1, ALU.is_ge)
        LOI = mk("mLOI", [[-1, L]], 1, ALU.is_ge)
        I64 = C.ident[0:L, 0:L]
        bc8 = lambda m: (m[:] if isinstance(m, Tl) else m).us(1).bc([L, H8, L])
        ones2 = C.ones[0:L, 0:2]

        def v3(t):
            return t[:].r("p (h x) -> p h x", x=L)

        inr = k.ring(3, [L, 6, 512], name="cin")
        big = k.ring(12, [L, 512], name="cw")
        bgb = k.ring(44, [L, 512], BF16, name="cwb")
        vbr = k.ring(3, [L, 512], BF16, name="vb")
        I64b = k.sb([L, L], BF16, name="i64b")
        k.copy(I64b[:], I64)
        SIG = {}
        SIGB = {}
        for b in range(NB):
            for d in range(2):
                SIG[(b, d)] = k.sb([L, 512], name="sig")
                k.memset(SIG[(b, d)][:], 0.0)
                SIGB[(b, d)] = k.sb([L, 512], BF16, name="sigb")
                k.memset(SIGB[(b, d)][:], 0.0)
        pl_r = k.ring(3, [L, H8, 2], name="pl")

        def mm8(lhs_fn, rhs_fn, ps, nacc=1):
            for h in range(H8):
                for i in range(nacc):
                    k.mm(ps[0:L, h * L:(h + 1) * L], lhs_fn(h, i), rhs_fn(h, i), start=(i == 0), stop=(i == nacc - 1))

        hs = lambda t, h: t[:, h * L:(h + 1) * L]

        for ci in range(NCK):
            for b in range(NB):
                for d in range(2):
                    c = ci if d == 0 else NCK - 1 - ci
                    g0 = b * S + c * L
                    MS_st, MS_ts, MI_st = (UP, LO, UPI) if d == 0 else (LO, UP, LOI)
                    X = inr.next()
                    for i, nm in enumerate(["rTok", "kTok", "vTok", "aTok", "bTok", "lw%dTok" % d]):
                        k.dma("sp", X[:, i, :], sc[nm][g0:g0 + L, :])
                    r_, k_, v_, a_, b_, lw_ = [X[:, i, :] for i in range(6)]
                    vb = vbr.next()
                    k.dma("pool", vb[:], sc["vTok"][g0:g0 + L, :])
                    v_ = vb[:]
                    ps_c = k.psn()
                    k.mm(ps_c[0:L, :], MI_st[:], lw_)
                    ps_r = k.psn()
                    k.mm(ps_r[0:L, :], MS_ts[:], lw_)
                    En, Ep, Er, Epv = big.next(), big.next(), big.next(), big.next()
                    k.act(En[:], ps_c[0:L, :], AF.Exp, scale=-1.0)
                    k.act(Ep[:], ps_c[0:L, :], AF.Exp)
                    k.tt(Epv[:], ps_c[0:L, :], lw_, ALU.subtract)
                    k.act(Epv[:], Epv[:], AF.Exp)
                    k.act(Er[:], ps_r[0:L, :], AF.Exp)
                    ps_p = k.psn()
                    for h in range(H8):
                        k.mm(ps_p[0:L, 2 * h:2 * h + 2], X[:, 5, h * L:(h + 1) * L], ones2)
                    PL = pl_r.next()
                    k.act(PL[:].r("p h x -> p (h x)"), ps_p[0:L, 0:2 * H8], AF.Exp)
                    at, bt, kt, rt = [big.next() for _ in range(4)]
                    Bh, Kh, atb = bgb.next(), bgb.next(), bgb.next()
                    k.tt(at[:], a_, Epv[:], ALU.mult)
                    k.tt(atb[:], a_, Epv[:], ALU.mult)
                    k.tt(bt[:], b_, En[:], ALU.mult)
                    k.tt(kt[:], k_, En[:], ALU.mult)
                    k.tt(rt[:], r_, Ep[:], ALU.mult)
                    k.tt(Bh[:], b_, Er[:], ALU.mult)
                    k.tt(Kh[:], k_, Er[:], ALU.mult)
                    FT = []
                    for src in (at, bt, kt, rt):
                        ps = k.psn()
                        for h in range(H8):
                            k.tr(ps[0:L, h * L:(h + 1) * L], hs(src, h), I64)
                        o = bgb.next()
                        k.evac(o[:], ps[0:L, :])
                        FT.append(o)
                    AT, BT, KT, RT = FT
                    def prod(lt, rt_, mask):
                        ps = k.psn()
                        mm8(lambda h, i: hs(lt, h), lambda h, i: hs(rt_, h), ps)
                        o = bgb.next()
                        k.tt(v3(o), ps[0:L, :].r("p (h x) -> p h x", x=L), bc8(mask), ALU.mult)
                        return o
                    Nn = prod(AT, BT, MS_ts)
                    Mm = prod(BT, AT, MS_st)
                    AakT = prod(KT, AT, MS_st)
                    ArbT = prod(BT, RT, MI_st)
                    ArkT = prod(KT, RT, MI_st)
                    Xt = bgb.next()
                    k.tt(v3(Xt), v3(Mm), I64b[:].us(1).bc([L, H8, L]), ALU.add)
                    Mp, Np = Mm, Nn
                    for lev in range(5):
                        ps_n = k.psn()
                        mm8(lambda h, i: hs(Mp, h), lambda h, i: hs(Np, h), ps_n)
                        Np2 = bgb.next()
                        k.evac(Np2[:], ps_n[0:L, :])
                        if lev < 4:
                            ps_m = k.psn()
                            mm8(lambda h, i: hs(Np, h), lambda h, i: hs(Mp, h), ps_m)
                            Mp2 = bgb.next()
                            k.evac(Mp2[:], ps_m[0:L, :])
                        else:
                            Mp2 = None
                        ps_x = k.psn()
                        mm8(lambda h, i: hs(Np2, h), lambda h, i: hs(Xt, h), ps_x)
                        Xn = bgb.next()
                        k.tt(Xn[:], ps_x[0:L, :], Xt[:], ALU.add)
                        Xt, Mp, Np = Xn, Mp2, Np2
                    ps = k.psn()
                    mm8(lambda h, i: hs(Xt, h), lambda h, i: hs(atb, h), ps)
                    TA = bgb.next()
                    k.evac(TA[:], ps[0:L, :])
                    ps = k.psn()
                    mm8(lambda h, i: hs(AakT, h), lambda h, i: v_[:, h * L:(h + 1) * L], ps)
                    W2 = bgb.next()
                    k.evac(W2[:], ps[0:L, :])
                    ps = k.psn()
                    mm8(lambda h, i: hs(Xt, h), lambda h, i: hs(W2, h), ps)
                    TAV = bgb.next()
                    k.evac(TAV[:], ps[0:L, :])
                    ps = k.psn()
                    mm8(lambda h, i: hs(TA, h), lambda h, i: hs(ArbT, h), ps)
                    QhT = bgb.next()
                    k.tt(QhT[:], ps[0:L, :], RT[:], ALU.add)
                    ps = k.psn()
                    mm8(lambda h, i: hs(ArbT if i == 0 else ArkT, h),
                        lambda h, i: (hs(TAV, h) if i == 0 else v_[:, h * L:(h + 1) * L]), ps, nacc=2)
                    Yloc = big.next()
                    k.evac(Yloc[:], ps[0:L, :])
                    ps = k.psn()
                    mm8(lambda h, i: hs(TA, h), lambda h, i: hs(Bh, h), ps)
                    Gd = big.next()
                    k.tt(v3(Gd), I64.us(1).bc([L, H8, L]), PL[:, :, 0].us(2).bc([L, H8, L]), ALU.mult)
                    GT = bgb.next()
                    k.tt(GT[:], ps[0:L, :], Gd[:], ALU.add)
                    ps = k.psn()
                    mm8(lambda h, i: hs(Bh if i == 0 else Kh, h),
                        lambda h, i: (hs(TAV, h) if i == 0 else v_[:, h * L:(h + 1) * L]), ps, nacc=2)
                    Hh = big.next()
                    k.evac(Hh[:], ps[0:L, :])
                    Sg = SIG[(b, d)]
                    Sgb = SIGB[(b, d)]
                    ps = k.psn()
                    mm8(lambda h, i: hs(QhT, h), lambda h, i: hs(Sgb, h), ps)
                    Y = big.next()
                    k.tt(Y[:], ps[0:L, :], Yloc[:], ALU.add)
                    k.dma("pool", sc["ypTok"][d * T + g0:d * T + g0 + L, :], Y[:])
                    ps = k.psn()
                    mm8(lambda h, i: hs(GT, h), lambda h, i: hs(Sgb, h), ps)
                    k.tt(Sg[:], ps[0:L, :], Hh[:], ALU.add)
                    k.copy(Sgb[:], Sg[:], "act")


def stage_rwkv_post(k, C):
    S, NB, T = C.S, C.NB, C.T
    sc = C.scr
    I = C.inp
    with k.scope():
        lnw = k.sb([128, 4], name="lnw")
        k.dma("sp", lnw[:], I["rw_ln_w"][0].r("(c p) -> p c", p=128))
        inr = k.ring(2, [128, 8, 512], name="pin")
        wr = k.ring(4, [128, 512], name="pw")
        sr = k.ring(4, [128, 8], name="ps8")
        ynr = k.ring(2, [128, 4, 512], name="yn")
        fr = k.ring(3, [128, 512], name="pf")
        for t0 in range(0, T, 512):
            yn = ynr.next()
            for j in range(4):
                g0 = t0 + j * 128
                X = inr.next()
                srcs = [sc["ypTok"][g0:g0 + 128, :], sc["ypTok"][T + g0:T + g0 + 128, :]]
                for i, s_ in enumerate(srcs):
                    k.dma("sp", X[:, i, :], s_)
                y = wr.next()
                k.tt(y[:], X[:, 0, :], X[:, 1, :], ALU.add)
                pr = wr.next()
                mean = sr.next()
                k.reduce(mean[:], y[:].r("p (h v) -> p h v", v=64))
                k.ts(mean[:], mean[:], 1.0 / 64, ALU.mult)
                k.tt(y[:].r("p (h v) -> p h v", v=64), y[:].r("p (h v) -> p h v", v=64),
                     mean[:].us(2).bc([128, 8, 64]), ALU.subtract)
                k.tt(pr[:], y[:], y[:], ALU.mult)
                var = sr.next()
                k.reduce(var[:], pr[:].r("p (h v) -> p h v", v=64))
                k.act(var[:], var[:], AF.Sqrt, bias=C.cst[:, 4:5], scale=1.0 / 64)
                k.recip(var[:], var[:])
                k.tt(yn[:, j, :].r("p (h v) -> p h v", v=64), y[:].r("p (h v) -> p h v", v=64),
                     var[:].us(2).bc([128, 8, 64]), ALU.mult)
            for cc in range(4):
                ps = k.psn()
                for j in range(4):
                    k.tr(ps[:, j * 128:(j + 1) * 128], yn[:, j, cc * 128:(cc + 1) * 128], C.ident[:])
                bon = fr.next()
                k.dma("sp", bon[:], sc["bonusT"][cc * 128:(cc + 1) * 128, t0:t0 + 512])
                gt = fr.next()
                k.dma("sp", gt[:], sc["gT"][cc * 128:(cc + 1) * 128, t0:t0 + 512])
                k.stt(bon[:], ps[:], lnw[:, cc:cc + 1], bon[:], ALU.mult, ALU.add)
                k.tt(bon[:], bon[:], gt[:], ALU.mult)
                k.dma("sp", C.mixT[cc * 128:(cc + 1) * 128, t0:t0 + 512], bon[:])


def dwconv_silu(k, C, src, row0, nch, wcv, bcv, dst, drow0, halo, outr):
    S, NB = C.S, C.NB
    for b in range(NB):
        for c in range(nch):
            hl = halo.next()
            k.memset(hl[:, 0:1], 0.0)
            k.memset(hl[:, S + 1:S + 2], 0.0)
            k.dma("sp", hl[:, 1:S + 1], src[row0 + c * 128:row0 + (c + 1) * 128, b * S:(b + 1) * S])
            o = outr.next()
            k.ts(o[:], hl[:, 0:S], wcv[:, c, 0:1], ALU.mult)
            k.stt(o[:], hl[:, 1:S + 1], wcv[:, c, 1:2], o[:], ALU.mult, ALU.add)
            k.stt(o[:], hl[:, 2:S + 2], wcv[:, c, 2:3], o[:], ALU.mult, ALU.add)
            k.act(o[:], o[:], AF.Silu, bias=bcv[:, c:c + 1])
            k.dma("sp", dst[drow0 + c * 128:drow0 + (c + 1) * 128, b * S:(b + 1) * S], o[:])


def gate_cumsums(k, C, la, nrow, half, X, tmp, ones_row, m01):
    S, NB = C.S, C.NB
    for b in range(NB):
        sl = slice(b * S, (b + 1) * S)
        k.scan(X[0:nrow, sl], ones_row[0:nrow, 0:S], la[0:nrow, sl])
        k.scan(V(tmp.t, tmp.ap[0:nrow, sl][:, ::-1]), ones_row[0:nrow, 0:S], V(la.t, la.ap[0:nrow, sl][:, ::-1]))
    k.ts(X[0:nrow, :], X[0:nrow, :], m01[0:nrow, 0:1], ALU.mult)
    k.stt(X[0:nrow, :], tmp[0:nrow, :], m01[0:nrow, 1:2], X[0:nrow, :], ALU.mult, ALU.add)


def rowmask(k, nrow, half, name):
    m = k.sb([nrow, 2], name=name)
    k.memset(m[:], 1.0)
    k.aselect(m[:, 0:1], m[:, 0:1], [[0, 1]], ALU.is_gt, 0.0, half, -1)
    k.aselect(m[:, 1:2], m[:, 1:2], [[0, 1]], ALU.is_ge, 0.0, -half, 1)
    return m


def decay_attention(k, C, nhead, hpg, kT_fn, qT_fn, v_fn, vw, X, bsT, nxT, lmT, rowof, mode, out_fn, mdt, nst=2, nxb=None):
    xd = C.scr["xrow"]
    if X is not None:
        k.dma("sp", xd[0:X.ap.shape[0], 0:C.S], X)
    S = C.S
    NI = S // 128
    NW = 4
    WW = NW * 128
    with k.scope():
        STr = k.ring(nst, [128, NI, WW], BF16, name="STs")
        XBr = k.ring(nxb or nst, [128, 2 * hpg, WW], name="Xb")
        Er = k.ring(3, [128, WW], BF16, name="E")
        Tr = k.ring(3, [128, 128], name="Tdiag")
        Edr = k.ring(3, [128, 128], BF16, name="Ed")
        Mdr = k.ring(3, [128, 128], mdt, name="Md")
        STmr = k.ring(nst, [128, NW, 2, 128], BF16, name="STm")
        Mr = k.ring(3, [128, WW], mdt, name="M")
        for g in range(nhead // hpg):
            for TJ in range(NI // NW):
                t0 = TJ * WW
                Js = [TJ * NW + jj for jj in range(NW)]
                STs = STr.next()
                for I in range(NI):
                    ps = k.psn()
                    k.mm(ps[:, 0:WW], kT_fn(g, I), qT_fn(g, t0, WW))
                    k.evac(STs[:, I, :], ps[:, 0:WW])
                XB = XBr.next()
                for ii in range(2 * hpg):
                    d, hl = ii // hpg, ii % hpg
                    row = rowof(d, g * hpg + hl)
                    k.dma("sp", XB[:, ii, :], xd[row, t0:t0 + WW].pb(128))
                STm = STmr.next()
                for jj, J in enumerate(Js):
                    for d in range(2):
                        k.tt(STm[:, jj, d, :], STs[:, J, jj * 128:(jj + 1) * 128], (C.mlow if d == 0 else C.mup)[:],
                             ALU.mult)
                for hl in range(hpg):
                    h = g * hpg + hl
                    for dset in ([(0, 1)] if mode in ("sum", "sumT") else [(0,), (1,)]):
                        items = []
                        touch = {J: 0 for J in Js}
                        for d in dset:
                            for I in range(NI):
                                pure = [jj for jj, J in enumerate(Js) if (d == 0 and I < J) or (d == 1 and I > J)]
                                dg = [jj for jj, J in enumerate(Js) if I == J]
                                if not pure and not dg:
                                    continue
                                items.append((d, I, pure, dg[0] if dg else None))
                                for jj in pure + dg:
                                    touch[Js[jj]] += 1
                        if mode == "sumT":
                            accT = k.psn()
                            po = (h % 2) * 64
                        else:
                            accs = {J: k.psn() for J in Js}
                        seen = {J: 0 for J in Js}
                        nmm = 0
                        tot_mm = sum((1 if it[2] else 0) + (1 if it[3] is not None else 0) for it in items)
                        for (d, I, pure, dg) in items:
                            row = rowof(d, h)
                            xb = XB[:, d * hpg + hl, :]
                            if pure:
                                c0, c1 = pure[0] * 128, (pure[-1] + 1) * 128
                                E = Er.next()
                                M = Mr.next()
                                k.act(E[:, c0:c1], xb[:, c0:c1], AF.Exp, bias=bsT[:, I, row:row + 1])
                                k.tt(M[:, c0:c1], E[:, c0:c1], STs[:, I, c0:c1], ALU.mult)
                            if dg is not None:
                                cs = slice(dg * 128, (dg + 1) * 128)
                                Tm = Tr.next()
                                Ed = Edr.next()
                                Md = Mdr.next()
                                k.ts(Tm[:], xb[:, cs], nxT[:, I, row:row + 1], ALU.add, 0.0, ALU.min)
                                k.act(Ed[:], Tm[:], AF.Exp, bias=lmT[:, I, row:row + 1])
                                k.tt(Md[:], Ed[:], STm[:, dg, d, :], ALU.mult)
                            if mode == "sumT":
                                if pure:
                                    k.mm(accT[po:po + vw, c0:c1], v_fn(h, I), M[:, c0:c1],
                                         start=(nmm == 0), stop=(nmm == tot_mm - 1))
                                    nmm += 1
                                if dg is not None:
                                    k.mm(accT[po:po + vw, cs], v_fn(h, I), Md[:],
                                         start=(nmm == 0), stop=(nmm == tot_mm - 1))
                                    nmm += 1
                                continue
                            for jj in pure:
                                J = Js[jj]
                                k.mm(accs[J][:, 0:vw], M[:, jj * 128:(jj + 1) * 128], v_fn(h, I),
                                     start=(seen[J] == 0), stop=(seen[J] == touch[J] - 1))
                                seen[J] += 1
                            if dg is not None:
                                J = Js[dg]
                                k.mm(accs[J][:, 0:vw], Md[:], v_fn(h, I),
                                     start=(seen[J] == 0), stop=(seen[J] == touch[J] - 1))
                                seen[J] += 1
                        if mode == "sumT":
                            out_fn(h, TJ, 0, accT)
                        else:
                            for J in Js:
                                out_fn(h, J, dset[0], accs[J])


def stage_mamba(k, C):
    S, NB, T = C.S, C.NB, C.T
    I = C.inp
    sc = C.scr
    NI = S // 128
    with k.scope():
        wcv = k.sb([128, 12, 3], name="wcv")
        for kk_ in range(3):
            k.dma("sp", wcv[:, :, kk_], I["mb_conv_w"][0, kk_].r("(c p) -> p c", p=128))
        bcv = k.sb([128, 12], name="bcv")
        k.dma("sp", bcv[:], I["mb_conv_b"][0].r("(c p) -> p c", p=128))
        halo = k.ring(2, [128, S + 2], name="chalo")
        outr = k.ring(2, [128, S], name="cout")
        dwconv_silu(k, C, C.colsT, 2816, 12, wcv[:], bcv[:], sc["xbcT"], 0, halo, outr)
    import os
    DBG = os.environ.get("KDBG", "")
    if "mb1" in DBG:
        return
    with k.scope():
        dtb = k.sb([32, 1], name="dtb")
        k.dma("sp", dtb[:], I["mb_dt_bias"][0].r("d h -> (d h)").us(1))
        k.ts(dtb[:], dtb[:], -1.0, ALU.mult)
        Aneg = k.sb([32, 1], name="Aneg")
        k.dma("sp", Aneg[:], I["mb_A_log"][0].r("d h -> (d h)").us(1))
        k.act(Aneg[:], Aneg[:], AF.Exp)
        k.ts(Aneg[:], Aneg[:], -1.0, ALU.mult)
        m01 = rowmask(k, 32, 16, "m01")
        bsT = k.sb([128, NI, 32], name="bsT")
        nxT = k.sb([128, NI, 32], name="nxT")
        lmT = k.sb([128, NI, 32], name="lmT")
        xs_tok = k.sb([128, NI, 1024], BF16, name="xs_tok")
        BC = k.sb([128, 4, S], BF16, name="BC")
        xsr = k.ring(2, [128, 8, 128], name="xsr")
        ytr = k.ring(3, [128, 512], name="yT")
        xtr = k.ring(3, [128, 512], name="xT")
        Dcol = k.sb([128, 8], name="Dcol")
        for h_ in range(16):
            k.dma("sp", Dcol[(h_ % 2) * 64:(h_ % 2) * 64 + 64, h_ // 2:h_ // 2 + 1], I["mb_D"][0][h_:h_ + 1].pb(64))
        for b in range(NB):
            bs0 = b * S
            with k.scope():
                onesr = k.sb([32, S], name="onesr")
                k.memset(onesr[:], 1.0)
                dt = k.sb([32, S], name="dt")
                tmp = k.sb([32, S], name="gtmp")
                la = k.sb([32, S], name="la")
                X = k.sb([32, S], name="X")
                k.dma("sp", dt[:], C.colsT[4352:4384, bs0:bs0 + S])
                k.ts(dt[:], dt[:], dtb[:, 0:1], ALU.subtract)
                k.ts(tmp[:], dt[:], -1.0, ALU.mult)
                k.tt(tmp[:], tmp[:], dt[:], ALU.min)
                k.act(tmp[:], tmp[:], AF.Exp)
                k.act(tmp[:], tmp[:], AF.Ln, bias=1.0)
                k.stt(dt[:], dt[:], 0.0, tmp[:], ALU.max, ALU.add)
                k.ts(la[:], dt[:], Aneg[:, 0:1], ALU.mult)
                k.act(dt[:], dt[:], AF.Ln)
                k.scan(X[:], onesr[:], la[:])
                k.scan(V(tmp, tmp.h[:, ::-1]), onesr[:], V(la, la.h[:, ::-1]))
                k.ts(X[:], X[:], m01[:, 0:1], ALU.mult)
                k.stt(X[:], tmp[:], m01[:, 1:2], X[:], ALU.mult, ALU.add)
                for src, dst in ((dt, lmT), (X, nxT)):
                    for j0 in range(0, NI, 16):
                        ps = k.psn()
                        n = min(16, NI - j0)
                        for j in range(j0, j0 + n):
                            k.tr(ps[:, (j - j0) * 32:(j - j0 + 1) * 32], src[0:32, j * 128:(j + 1) * 128], C.ident[0:32, 0:32])
                        k.evac(dst[:, j0:j0 + n, :], ps[:, 0:n * 32].r("p (j r) -> p j r", r=32))

                k.dma("sp", sc["xrow"][0:32, 0:S], X[:])
            k.ts(nxT[:], nxT[:], -1.0, ALU.mult)
            k.tt(bsT[:], lmT[:], nxT[:], ALU.add)
            for j in range(NI):
                xin = xsr.next()
                k.dma("sp", xin[:], sc["xbcT"][0:1024, bs0 + j * 128:bs0 + (j + 1) * 128].r("(c p) t -> p c t", p=128))
                for c0 in (0, 4):
                    ps = k.psn()
                    for c in range(c0, c0 + 4):
                        k.tr(ps[:, (c - c0) * 128:(c - c0 + 1) * 128], xin[:, c, :], C.ident[:])
                    k.evac(xs_tok[:, j, c0 * 128:(c0 + 4) * 128], ps[:])
            k.dma("pool", BC[:], sc["xbcT"][1024:1536, bs0:bs0 + S].r("(c p) t -> p c t", p=128))
            ystate = {}
            if "mb2" in DBG:
                continue

            def out_fn(h, TJ, d, acc):
                c = h // 2
                po = (h % 2) * 64
                t0 = bs0 + TJ * 512
                if h % 2 == 0:
                    ystate["y"] = ytr.next()
                    ystate["x"] = xtr.next()
                    k.dma("sp", ystate["x"][:], sc["xbcT"][c * 128:(c + 1) * 128, t0:t0 + 512])
                y, xT = ystate["y"], ystate["x"]
                k.stt(y[po:po + 64, :], xT[po:po + 64, :], Dcol[po:po + 64, c:c + 1], acc[po:po + 64, 0:512],
                      ALU.mult, ALU.add)
                if h % 2 == 1:
                    k.dma("pool", sc["ymbT"][c * 128:(c + 1) * 128, t0:t0 + 512], y[:])

            decay_attention(
                k, C, 16, 8,
                lambda g, Ib: BC[:, g, Ib * 128:(Ib + 1) * 128],
                lambda g, t0, n: BC[:, 2 + g, t0:t0 + n],
                lambda h, Ib: xs_tok[:, Ib, h * 64:(h + 1) * 64], 64,
                None, bsT, nxT, lmT, lambda d, h: d * 16 + h, "sumT", out_fn, BF16)
    with k.scope():
        nw = k.sb([128, 8], name="mbnw")
        k.dma("sp", nw[:], I["mb_norm_w"][0].r("(c p) -> p c", p=128))
        yr = k.ring(2, [128, 8, 512], name="ytk")
        zr = k.ring(2, [128, 8, 512], name="z")
        gr = k.ring(2, [128, 8, 512], name="gy")
        sqr = k.ring(3, [128, 512], name="gsq")
        for t0 in range(0, T, 512):
            yt = yr.next()
            k.dma("sp", yt[:], sc["ymbT"][:, t0:t0 + 512].r("(c p) t -> p c t", p=128))
            z = zr.next()
            k.dma("sp", z[:], C.colsT[1792:2816, t0:t0 + 512].r("(c p) t -> p c t", p=128))
            k.act(z[:], z[:], AF.Silu)
            gy = gr.next()
            for c in range(8):
                k.tt(gy[:, c, :], yt[:, c, :], z[:, c, :], ALU.mult)
            for g in range(2):
                ps = k.psn()
                for c in range(4):
                    sq = sqr.next()
                    k.tt(sq[:], gy[:, g * 4 + c, :], gy[:, g * 4 + c, :], ALU.mult)
                    k.mm(ps[:], C.ones[:], sq[:], start=(c == 0), stop=(c == 3))
                rs = sqr.next()
                k.act(rs[:], ps[:], AF.Sqrt, bias=C.cst[:, 3:4], scale=1.0 / 512)
                k.recip(rs[:], rs[:])
                for c in range(4):
                    cc = g * 4 + c
                    k.stt(gy[:, cc, :], gy[:, cc, :], nw[:, cc:cc + 1], rs[:], ALU.mult, ALU.mult)
            k.dma("sp", C.mixT[512:1536, t0:t0 + 512].r("(c p) t -> p c t", p=128), gy[:])


def stage_s5(k, C):
    S, NB, T = C.S, C.NB, C.T
    I = C.inp
    sc = C.scr
    with k.scope():
        P = {}
        for nm in ("ar", "ai", "ldt"):
            P[nm] = k.sb([128, 32], name="s5" + nm)
        with k.nc.allow_non_contiguous_dma(reason="tiny param transposes"):
            for d in range(2):
                for gg in range(2):
                    k.dma("sp", P["ar"][gg * 64:(gg + 1) * 64, d * 16:(d + 1) * 16],
                          I["s5_A_re"][0, d].r("(j gg) p -> gg p j", gg=2)[gg])
                    k.dma("sp", P["ai"][gg * 64:(gg + 1) * 64, d * 16:(d + 1) * 16],
                          I["s5_A_im"][0, d].r("(j gg) p -> gg p j", gg=2)[gg])
                    k.dma("sp", P["ldt"][gg * 64:(gg + 1) * 64, d * 16:(d + 1) * 16],
                          I["s5_log_dt"][0, d].r("(j gg) -> gg j", gg=2)[gg].pb(64))
        sm = lambda nm: k.sb([128, 32], name="s5" + nm)
        dtt, th, mag, c_, s_, t1, t2 = [sm(n) for n in ("dt", "th", "mag", "c", "s", "t1", "t2")]
        k.act(dtt[:], P["ldt"][:], AF.Exp)
        k.ts(P["ar"][:], P["ar"][:], -1e-4, ALU.min)
        k.tt(th[:], dtt[:], P["ai"][:], ALU.mult)
        k.tt(mag[:], dtt[:], P["ar"][:], ALU.mult)
        k.act(mag[:], mag[:], AF.Exp)
        k.act(s_[:], th[:], AF.Sin, scale=1.0 / 16)
        k.act(t1[:], th[:], AF.Sin, scale=1.0 / 32)
        k.tt(t1[:], t1[:], t1[:], ALU.mult)
        k.ts(c_[:], t1[:], -2.0, ALU.mult, 1.0, ALU.add)
        for _ in range(4):
            k.tt(t1[:], c_[:], c_[:], ALU.mult)
            k.tt(t2[:], s_[:], s_[:], ALU.mult)
            k.tt(s_[:], s_[:], c_[:], ALU.mult)
            k.ts(s_[:], s_[:], 2.0, ALU.mult)
            k.tt(c_[:], t1[:], t2[:], ALU.subtract)
        NLEV = max(1, int(math.log2(S)))
        PW = k.sb([128, NLEV + 1, 3, 32], name="s5pw")
        k.copy(PW[:, 0, 0, :], c_[:])
        k.copy(PW[:, 0, 1, :], s_[:])
        for lv in range(1, NLEV + 1):
            k.tt(t1[:], PW[:, lv - 1, 0, :], PW[:, lv - 1, 0, :], ALU.mult)
            k.tt(t2[:], PW[:, lv - 1, 1, :], PW[:, lv - 1, 1, :], ALU.mult)
            k.tt(PW[:, lv, 0, :], t1[:], t2[:], ALU.subtract)
            k.tt(t1[:], PW[:, lv - 1, 0, :], PW[:, lv - 1, 1, :], ALU.mult)
            k.ts(PW[:, lv, 1, :], t1[:], 2.0, ALU.mult)
        k.ts(PW[:, :, 2, :], PW[:, :, 1, :], -1.0, ALU.mult)
        abr, abi, den, fr_, fi_ = [sm(n) for n in ("abr", "abi", "den", "fr", "fi")]
        k.tt(abr[:], mag[:], c_[:], ALU.mult)
        k.tt(abi[:], mag[:], s_[:], ALU.mult)
        k.tt(den[:], P["ar"][:], P["ar"][:], ALU.mult)
        k.tt(t1[:], P["ai"][:], P["ai"][:], ALU.mult)
        k.tt(den[:], den[:], t1[:], ALU.add)
        k.recip(den[:], den[:])
        k.ts(t1[:], abr[:], -1.0, ALU.add)
        k.tt(fr_[:], t1[:], P["ar"][:], ALU.mult)
        k.tt(t2[:], abi[:], P["ai"][:], ALU.mult)
        k.tt(fr_[:], fr_[:], t2[:], ALU.add)
        k.tt(fr_[:], fr_[:], den[:], ALU.mult)
        k.tt(fi_[:], abi[:], P["ar"][:], ALU.mult)
        k.tt(t2[:], t1[:], P["ai"][:], ALU.mult)
        k.tt(fi_[:], fi_[:], t2[:], ALU.subtract)
        k.tt(fi_[:], fi_[:], den[:], ALU.mult)
        Bre = k.sb([128, 16, 16], name="Bre")
        Bim = k.sb([128, 16, 16], name="Bim")
        for gg in range(2):
            k.dma("sp", Bre[gg * 64:(gg + 1) * 64], I["s5_B_re"][0].r("(j gg) p m -> gg p j m", gg=2)[gg])
            k.dma("sp", Bim[gg * 64:(gg + 1) * 64], I["s5_B_im"][0].r("(j gg) p m -> gg p j m", gg=2)[gg])
        BT = k.sb([128, 2, 2, 16, 128], name="BT")
        bbr = k.ring(2, [128, 16], name="bb")
        BDr = k.ring(2, [128, 4, 2, 16], name="BD")
        for d in range(2):
            for ri in range(2):
                for j in range(16):
                    q = j % 4
                    cj = d * 16 + j
                    BD = BDr.next()
                    k.memset(BD[:], 0.0)
                    bb = bbr.next()
                    if ri == 0:
                        k.ts(bb[:], Bim[:, j, :], fi_[:, cj:cj + 1], ALU.mult)
                        k.stt(bb[:], Bre[:, j, :], fr_[:, cj:cj + 1], bb[:], ALU.mult, ALU.subtract)
                    else:
                        k.ts(bb[:], Bre[:, j, :], fi_[:, cj:cj + 1], ALU.mult)
                        k.stt(bb[:], Bim[:, j, :], fr_[:, cj:cj + 1], bb[:], ALU.mult, ALU.add)
                    for gg in range(2):
                        k.ts(BD[:, q, gg, :], bb[:], C.hm[:, gg:gg + 1], ALU.mult)
                    ps = k.psn()
                    k.tr(ps[:, 0:128], BD[:].r("p q g m -> p (q g m)"), C.ident[:])
                    k.evac(BT[:, d, ri, j, :], ps[:, 0:128])
        CT = k.sb([128, 2, 2, 8, 128], BF16, name="CTz")
        k.memset(CT[:], 0.0)
        cinr = k.ring(2, [128, 128], name="cin")
        ctr = k.ring(2, [128, 128], name="ctf")

        def build_CT(c4):
            for d in range(2):
                for ri, nm in enumerate(("s5_C_re", "s5_C_im")):
                    cin = cinr.next()
                    src = I[nm][0, d].r("g m p -> (g m) p")[c4 * 128:(c4 + 1) * 128, :]
                    k.dma("sp", cin[:, 0:64], src)
                    k.dma("sp", cin[:, 64:128], src)
                    ps = k.psn()
                    k.tr(ps[:, 0:128], cin[:], C.ident[:])
                    ctf = ctr.next()
                    k.evac(ctf[:], ps[:, 0:128])
                    for g8 in range(8):
                        gg = g8 % 2
                        k.ts(CT[gg * 64:(gg + 1) * 64, d, ri, g8, g8 * 16:(g8 + 1) * 16],
                             ctf[gg * 64:(gg + 1) * 64, g8 * 16:(g8 + 1) * 16], (1.0 if ri == 0 else -1.0), ALU.mult)
        dsk = k.sb([128, 4], name="s5D")
        k.dma("sp", dsk[:], I["s5_D"][0].r("(c p) -> p c", p=128))
        uT = k.sb([128, T], name="uT")
        yacc = k.sb([128, T], name="yacc")
        Tcr = k.ring(2, [128, S], name="Tc")
        Tsr = k.ring(2, [128, S], name="Tsn")
        tbig_p = k.ring(2, [128, S // 2], name="s5bigp")
        magt = k.sb([128, S], name="magt")
        ones_s = k.sb([128, S], name="ones_s")
        k.memset(ones_s[:], 1.0)
        br_ = k.sb([128, S], name="btr")
        bi_ = k.sb([128, S], name="bti")
        wr_ = k.sb([128, S], BF16, name="wre")
        wi_ = k.sb([128, S], BF16, name="wim")
        tmr = k.ring(3, [128, 512], name="s5tmp")
        tbig = k.ring(2, [128, S], name="s5big")
        for c4 in range(4):
            build_CT(c4)
            k.dma("sp", uT[:], C.colsT[c4 * 128:(c4 + 1) * 128, :])
            first = True
            for d in range(2):
                for q in range(4):
                    j = c4 * 4 + q
                    cj = d * 16 + j
                    Tc, Ts = Tcr.next(), Tsr.next()
                    k.memset(Tc[:, 0:1], 1.0)
                    k.memset(Ts[:, 0:1], 0.0)
                    n = 1
                    lv = 0
                    while n < S:
                        pc, ps_, pns = [PW[:, lv, i_, cj:cj + 1] for i_ in range(3)]
                        tb = tbig_p.next()
                        k.ts(tb[:, 0:n], Ts[:, 0:n], pns, ALU.mult, eng="pool")
                        k.ts(Tc[:, n:2 * n], Tc[:, 0:n], pc, ALU.mult, eng="pool")
                        k.tt(Tc[:, n:2 * n], Tc[:, n:2 * n], tb[:, 0:n], ALU.add, eng="pool")
                        tb2 = tbig_p.next()
                        k.ts(tb2[:, 0:n], Tc[:, 0:n], ps_, ALU.mult, eng="pool")
                        k.ts(Ts[:, n:2 * n], Ts[:, 0:n], pc, ALU.mult, eng="pool")
                        k.tt(Ts[:, n:2 * n], Ts[:, n:2 * n], tb2[:, 0:n], ALU.add, eng="pool")
                        n *= 2
                        lv += 1
                    k.ts(magt[:], ones_s[:], mag[:, cj:cj + 1], ALU.mult)
                    for b in range(NB):
                        rev = (lambda v: V(v.t, v.ap[:, ::-1])) if d == 1 else (lambda v: v)
                        for t0 in range(0, S, 512):
                            tw = min(512, S - t0)
                            g0 = b * S + t0
                            psr = k.psn()
                            k.mm(psr[:, 0:tw], BT[:, d, 0, j, :], uT[:, g0:g0 + tw])
                            psi = k.psn()
                            k.mm(psi[:, 0:tw], BT[:, d, 1, j, :], uT[:, g0:g0 + tw])
                            if d == 0:
                                tc, tsn = Tc[:, t0:t0 + tw], Ts[:, t0:t0 + tw]
                            else:
                                tc = V(Tc, Tc.h[:, S - t0 - tw:S - t0][:, ::-1])
                                tsn = V(Ts, Ts.h[:, S - t0 - tw:S - t0][:, ::-1])
                            a1, a2_ = tmr.next(), tmr.next()
                            k.tt(a1[:, 0:tw], psr[:, 0:tw], tc, ALU.mult)
                            k.tt(a2_[:, 0:tw], psi[:, 0:tw], tsn, ALU.mult)
                            k.tt(br_[:, t0:t0 + tw], a1[:, 0:tw], a2_[:, 0:tw], ALU.add)
                            k.tt(a1[:, 0:tw], psi[:, 0:tw], tc, ALU.mult)
                            k.tt(a2_[:, 0:tw], psr[:, 0:tw], tsn, ALU.mult)
                            k.tt(bi_[:, t0:t0 + tw], a1[:, 0:tw], a2_[:, 0:tw], ALU.subtract)
                        k.scan(rev(br_[:]), magt[:], rev(br_[:]))
                        k.scan(rev(bi_[:]), magt[:], rev(bi_[:]))
                        tcf = Tc[:] if d == 0 else V(Tc, Tc.h[:, ::-1])
                        tsf = Ts[:] if d == 0 else V(Ts, Ts.h[:, ::-1])
                        b1, b2 = tbig.next(), tbig.next()
                        k.tt(b1[:], br_[:], tcf, ALU.mult)
                        k.tt(b2[:], bi_[:], tsf, ALU.mult, eng="pool")
                        k.tt(wr_[:], b1[:], b2[:], ALU.subtract)
                        b1, b2 = tbig.next(), tbig.next()
                        k.tt(b1[:], br_[:], tsf, ALU.mult)
                        k.tt(b2[:], bi_[:], tcf, ALU.mult, eng="pool")
                        k.tt(wi_[:], b1[:], b2[:], ALU.add)
                        for t0 in range(0, S, 512):
                            tw = min(512, S - t0)
                            g0 = b * S + t0
                            ps = k.psn()
                            n_mm = 0
                            for gg in range(2):
                                g8 = (2 * j + gg) % 8
                                for ri, wt in ((0, wr_), (1, wi_)):
                                    k.mm(ps[:, 0:tw], CT[:, d, ri, g8, :],
                                         wt[:, t0:t0 + tw], start=(n_mm == 0), stop=(n_mm == 3))
                                    n_mm += 1
                            if first:
                                k.evac(yacc[:, g0:g0 + tw], ps[:, 0:tw])
                            else:
                                k.tt(yacc[:, g0:g0 + tw], yacc[:, g0:g0 + tw], ps[:, 0:tw], ALU.add)
                    first = False
            for t0 in range(0, T, 512):
                y = tmr.next()
                k.stt(y[:], uT[:, t0:t0 + 512], dsk[:, c4:c4 + 1], yacc[:, t0:t0 + 512], ALU.mult, ALU.add)
                x2 = tmr.next()
                k.tt(x2[:], y[:], y[:], ALU.mult)
                k.ts(x2[:], x2[:], 0.044715, ALU.mult, 1.0, ALU.add)
                k.tt(x2[:], x2[:], y[:], ALU.mult)
                k.act(x2[:], x2[:], AF.Sigmoid, scale=2.0 * math.sqrt(2.0 / math.pi))
                k.tt(y[:], y[:], x2[:], ALU.mult)
                k.dma("sp", sc["s5yT"][c4 * 128:(c4 + 1) * 128, t0:t0 + 512], y[:])
    TB = C.TB
    with k.scope():
        xin = k.sb([128, 4, TB], MMDT, name="gluin")
        gb = k.sb([128, 4], name="glub")
        k.dma("sp", gb[:], I["s5_glu_b"][0].r("(c p) -> p c", p=128))
        wring = k.ring(2, [128, 4, 512], MMDT, name="wglu")
        yr = k.ring(3, [128, 512], name="gluy")
        for tb0 in range(0, T, TB):
            k.dma("pool", xin[:], sc["s5yT"][:, tb0:tb0 + TB].r("(c p) t -> p c t", p=128))

            def epi(n, mw, t0, ps):
                yv = yr.next()
                k.dma("sp", yv[:], sc["s5yT"][n:n + 128, tb0 + t0:tb0 + t0 + 512])
                sg = yr.next()
                k.act(sg[:], ps[:], AF.Sigmoid, bias=gb[:, n // 128:n // 128 + 1])
                k.tt(yv[:], yv[:], sg[:], ALU.mult)
                k.dma("sp", C.mixT[n:n + 128, tb0 + t0:tb0 + t0 + 512], yv[:])
            dense(k, xin[:], 4, I["s5_glu_w"][0], 512, TB, epi, wring, 512)


def stage_mlstm(k, C):
    S, NB, T = C.S, C.NB, C.T
    I = C.inp
    sc = C.scr
    NI = S // 128
    with k.scope():
        wcv = k.sb([128, 8, 3], name="wcv")
        for kk_ in range(3):
            k.dma("sp", wcv[:, :, kk_], I["ml_conv_w"][0, kk_].r("(c p) -> p c", p=128))
        bcv = k.sb([128, 8], name="bcv")
        k.dma("sp", bcv[:], I["ml_conv_b"][0].r("(c p) -> p c", p=128))
        halo = k.ring(2, [128, S + 2], name="chalo")
        outr = k.ring(2, [128, S], name="cout")
        dwconv_silu(k, C, C.colsT, 512, 8, wcv[:], bcv[:], sc["xcT"], 0, halo, outr)
    with k.scope():
        m32 = k.sb([128, 32], name="m32")
        k.memset(m32[:], 1.0)
        k.aselect(m32[:], m32[:], [[-4, 32]], ALU.is_ge, 0.0, 0, 1)
        k.aselect(m32[:], m32[:], [[4, 32]], ALU.is_ge, 0.0, 3, -1)
        WB = k.sb([128, 3, 8, 128], name="WB")
        wl = k.sb([128, 3, 8, 4], name="wl")
        with k.nc.allow_non_contiguous_dma(reason="tiny weights"):
            for i, nm in enumerate(("ml_wq", "ml_wk", "ml_wv")):
                k.dma("sp", wl[:, i], I[nm][0].r("(c j) a d -> (j a) c d", j=32))
        k.ts(wl[:, 1], wl[:, 1], 128.0 ** -0.5, ALU.mult)
        for i in range(3):
            for c in range(8):
                k.tt(WB[:, i, c, :].r("p (j d) -> p j d", d=4), wl[:, i, c, :].us(1).bc([128, 32, 4]),
                     m32[:].us(2).bc([128, 32, 4]), ALU.mult)
        ib = k.sb([16, 1], name="ib")
        k.dma("sp", ib[:], I["ml_i_b"][0].r("d h -> (d h)").us(1))
        fbn = k.sb([16, 1], name="fbn")
        k.dma("sp", fbn[:], I["ml_f_b"][0].r("d h -> (d h)").us(1))
        k.ts(fbn[:], fbn[:], -1.0, ALU.mult)
        m01 = rowmask(k, 16, 8, "m01l")
        nwb = k.sb([128, 1024], name="nwb")
        k.dma("sp", nwb[:], I["ml_norm_w"][0].pb(128))
        qT = k.sb([128, 8, S], BF16, name="qT")
        kT = k.sb([128, 8, S], BF16, name="kT")
        vtok = k.sb([128, NI, 8, 130], BF16, name="vtok")
        k.memset(vtok[:, :, :, 128:129], 1.0)
        lmT = k.sb([128, NI, 16], name="lmT")
        nxT = k.sb([128, NI, 16], name="nxT")
        bsT = k.sb([128, NI, 16], name="bsT")
        xr = k.ring(2, [128, 512], name="mlx")
        vr = k.ring(1, [128, S], name="mlv")
        htr = k.ring(4, [128, 128], name="h0")
        hor = k.ring(3, [128, 128], name="ho")
        s1 = k.ring(6, [128, 1], name="s1")
        for b in range(NB):
            for c in range(8):
                vT = vr.next()
                for t0 in range(0, S, 512):
                    tw = min(512, S - t0)
                    g0 = b * S + t0
                    xc = xr.next()
                    k.dma("sp", xc[:, 0:tw], sc["xcT"][c * 128:(c + 1) * 128, g0:g0 + tw])
                    xm = xr.next()
                    k.dma("sp", xm[:, 0:tw], C.colsT[512 + c * 128:512 + (c + 1) * 128, g0:g0 + tw])
                    ps = k.psn()
                    k.mm(ps[:, 0:tw], WB[:, 0, c, :], xc[:, 0:tw])
                    k.evac(qT[:, c, t0:t0 + tw], ps[:, 0:tw])
                    ps = k.psn()
                    k.mm(ps[:, 0:tw], WB[:, 1, c, :], xc[:, 0:tw])
                    k.evac(kT[:, c, t0:t0 + tw], ps[:, 0:tw])
                    ps = k.psn()
                    k.mm(ps[:, 0:tw], WB[:, 2, c, :], xm[:, 0:tw])
                    k.evac(vT[:, t0:t0 + tw], ps[:, 0:tw])
                for i0 in range(0, NI, 4):
                    n = min(4, NI - i0)
                    ps = k.psn()
                    for i in range(i0, i0 + n):
                        k.tr(ps[:, (i - i0) * 128:(i - i0 + 1) * 128], vT[:, i * 128:(i + 1) * 128], C.ident[:])
                    k.evac(vtok[:, i0:i0 + n, c, 0:128], ps[:, 0:n * 128].r("p (i f) -> p i f", f=128))
            with k.scope():
                li = k.sb([16, S], name="li")
                lf = k.sb([16, S], name="lf")
                X = k.sb([16, S], name="Xl")
                onesr = k.sb([16, S], name="onesr")
                k.memset(onesr[:], 1.0)
                def tr16(src, dst):
                    for j0 in range(0, NI, 32):
                        ps = k.psn()
                        n = min(32, NI - j0)
                        for j in range(j0, j0 + n):
                            k.tr(ps[:, (j - j0) * 16:(j - j0 + 1) * 16], src[0:16, j * 128:(j + 1) * 128], C.ident[0:16, 0:16])
                        k.evac(dst[:, j0:j0 + n, :], ps[:, 0:n * 16].r("p (j r) -> p j r", r=16))
                k.dma("sp", li[:], C.colsT[2560:2576, b * S:(b + 1) * S])
                k.dma("sp", lf[:], C.colsT[2576:2592, b * S:(b + 1) * S])
                k.ts(li[:], li[:], ib[:, 0:1], ALU.add)
                tr16(li, lmT)
                tmp = li
                softplus_neg(k, lf[:], lf[:], fbn[:, 0:1], tmp[:])
                k.ts(lf[:], lf[:], -1.0, ALU.mult)
                k.scan(X[:], onesr[:], lf[:])
                k.scan(V(tmp, tmp.h[:, ::-1]), onesr[:], V(lf, lf.h[:, ::-1]))
                k.ts(X[:], X[:], m01[:, 0:1], ALU.mult)
                k.stt(X[:], tmp[:], m01[:, 1:2], X[:], ALU.mult, ALU.add)
                tr16(X, nxT)

                k.dma("sp", sc["xrow"][0:16, 0:S], X[:])
            k.ts(nxT[:], nxT[:], -1.0, ALU.mult)
            k.tt(bsT[:], lmT[:], nxT[:], ALU.add)
            hstate = {}

            def out_fn(h, J, d, acc):
                den = s1.next()
                k.act(den[:], acc[:, 128:129], AF.Abs)
                k.ts(den[:], den[:], 1.0, ALU.max)
                k.recip(den[:], den[:])
                if d == 0:
                    h0 = htr.next()
                    hstate[J] = h0
                    k.ts(h0[:], acc[:, 0:128], den[:, 0:1], ALU.mult)
                    return
                h0 = hstate[J]
                k.stt(h0[:], acc[:, 0:128], den[:, 0:1], h0[:], ALU.mult, ALU.add)
                mean = s1.next()
                k.reduce(mean[:], h0[:])
                k.ts(mean[:], mean[:], 1.0 / 128, ALU.mult)
                k.ts(h0[:], h0[:], mean[:, 0:1], ALU.subtract)
                ho = hor.next()
                var = s1.next()
                k.tt(ho[:], h0[:], h0[:], ALU.mult)
                k.reduce(var[:], ho[:])
                k.act(var[:], var[:], AF.Sqrt, bias=C.cst[:, 3:4], scale=1.0 / 128)
                k.recip(var[:], var[:])
                k.stt(ho[:], h0[:], var[:, 0:1], nwb[:, h * 128:(h + 1) * 128], ALU.mult, ALU.mult)
                g0 = b * S + J * 128
                k.dma("pool", sc["hmlTok"][g0:g0 + 128, h * 128:(h + 1) * 128], ho[:])

            decay_attention(
                k, C, 8, 1,
                lambda g, Ib: kT[:, g, Ib * 128:(Ib + 1) * 128],
                lambda g, t0, n: qT[:, g, t0:t0 + n],
                lambda h, Ib: vtok[:, Ib, h, 0:129], 129,
                None, bsT, nxT, lmT, lambda d, h: d * 8 + h, "den", out_fn, BF16, nst=1)
    with k.scope():
        skp = k.sb([128, 8], name="skp")
        k.dma("sp", skp[:], I["ml_skip"][0].r("(c p) -> p c", p=128))
        hr_ = k.ring(2, [128, 4, 1024], name="htk")
        orr = k.ring(2, [128, 8, 512], name="og")
        xr2 = k.ring(2, [128, 8, 512], name="xc2")
        for t0 in range(0, T, 512):
            ht = hr_.next()
            k.dma("sp", ht[:], sc["hmlTok"][t0:t0 + 512, :].r("(j p) f -> p j f", p=128))
            og = orr.next()
            k.dma("sp", og[:], C.colsT[1536:2560, t0:t0 + 512].r("(c p) t -> p c t", p=128))
            k.act(og[:], og[:], AF.Sigmoid)
            xc = xr2.next()
            k.dma("sp", xc[:], sc["xcT"][:, t0:t0 + 512].r("(c p) t -> p c t", p=128))
            for c in range(8):
                ps = k.psn()
                for j in range(4):
                    k.tr(ps[:, j * 128:(j + 1) * 128], ht[:, j, c * 128:(c + 1) * 128], C.ident[:])
                k.tt(og[:, c, :], og[:, c, :], ps[:], ALU.mult)
                k.stt(og[:, c, :], xc[:, c, :], skp[:, c:c + 1], og[:, c, :], ALU.mult, ALU.add)
            k.dma("sp", C.mixT[512:1536, t0:t0 + 512].r("(c p) t -> p c t", p=128), og[:])


def build(S, NB=2, stages=None, dbg=()):
    T = NB * S
    nc = bass.Bass("TRN2", target_bir_lowering=False)
    C = Ctx()
    C.S, C.NB, C.T = S, NB, T
    C.TB = min(1024, T)
    x = nc.dram_tensor("x", [NB, S, D], F32, kind="ExternalInput").ap()
    out = nc.dram_tensor("out", [NB, S, D], F32, kind="ExternalOutput").ap()
    C.x = Tl(x, "x").v()
    C.out = Tl(out, "out", acc=True).v()
    C.inp = {}
    for nm, shp in IN_SPECS:
        C.inp[nm] = Tl(nc.dram_tensor(nm, list(shp), F32, kind="ExternalInput").ap(), nm)
    with ExitStack() as es:
        es.enter_context(nc.allow_non_contiguous_dma(reason="small parameter layouts"))
        k = K(nc, es)

        def scr(name, shape):
            if name in dbg:
                return Tl(nc.dram_tensor(name, list(shape), F32, kind="ExternalOutput").ap(), name, acc=True)
            return k.dram(name, shape)
        C.hT = scr("hT", [D, T])
        C.colsT = scr("colsT", [AB_IN, T])
        C.mixT = scr("mixT", [1536, T])
        C.scr = {}
        for nm, shp in [("decT", [1024, T]), ("rwT", [1024, T]), ("aT", [512, T]), ("gT", [512, T]),
                        ("bonusT", [512, T]), ("rTok", [T, 512]), ("bTok", [T, 512]), ("kTok", [T, 512]),
                        ("vTok", [T, 512]), ("aTok", [T, 512]), ("lw0Tok", [T, 512]), ("lw1Tok", [T, 512]), ("saTok", [2 * T, 512]), ("ypTok", [2 * T, 512]),
                        ("xbcT", [1536, T]), ("ymbT", [1024, T]), ("s5yT", [512, T]), ("xcT", [1024, T]),
                        ("hmlTok", [T, 1024]), ("xrow", [32, S])]:
            C.scr[nm] = scr(nm, shp)
        stage_consts(k, C)
        st = stages or ["load", "in0", "rwkv", "mamba", "out0", "mlp0", "in1", "s5", "mlstm", "out1", "mlp1", "final"]
        for s_ in st:
            if s_ == "load":
                stage_load_x(k, C)
            elif s_ == "in0":
                stage_inproj(k, C, 0, C.inp["ab_w_in"][0], AB_IN, C.inp["norm_mix"][0].r("(c p) -> p c", p=128))
            elif s_ == "rwkv":
                stage_rwkv_prep(k, C)
                stage_rwkv_scan(k, C)
                stage_rwkv_post(k, C)
            elif s_ == "rwprep":
                stage_rwkv_prep(k, C)
            elif s_ == "rwscan":
                stage_rwkv_scan(k, C)
            elif s_ == "rwpost":
                stage_rwkv_post(k, C)
            elif s_ == "mamba":
                stage_mamba(k, C)
            elif s_ == "out0":
                stage_outproj(k, C, C.inp["ab_w_out"][0])
            elif s_ == "mlp0":
                stage_mlp(k, C, 0)
            elif s_ == "in1":
                stage_inproj(k, C, 1, C.inp["cd_w_in"][0], CD_IN, C.inp["norm_mix"][1].r("(c p) -> p c", p=128))
            elif s_ == "s5":
                stage_s5(k, C)
            elif s_ == "mlstm":
                stage_mlstm(k, C)
            elif s_ == "out1":
                stage_outproj(k, C, C.inp["cd_w_out"][0])
            elif s_ == "mlp1":
                stage_mlp(k, C, 1)
            elif s_ == "final":
                stage_final(k, C)
        k.finish()
    return nc


_NC_CACHE = {}


def kernel(**inputs):
    x = np.ascontiguousarray(np.asarray(inputs["x"], dtype=np.float32))
    B, S, _ = x.shape
    ncores = 8
    NB = B // ncores
    key = (S, NB)
    if key not in _NC_CACHE:
        _NC_CACHE[key] = build(S, NB)
    nc = _NC_CACHE[key]
    shared = {nm: np.ascontiguousarray(np.asarray(inputs[nm], dtype=np.float32)) for nm, _ in IN_SPECS}
    in_maps = []
    for i in range(ncores):
        m = dict(shared)
        m["x"] = x[i * NB:(i + 1) * NB]
        in_maps.append(m)
    res = run_bass_kernel_spmd(nc, in_maps, core_ids=list(range(ncores)))
    return np.concatenate([r["out"] for r in res.results], axis=0).astype(np.float32)
```

```python
import math
import numpy as np
from contextlib import ExitStack, contextmanager
import concourse.bass as bass
import concourse.mybir as mybir
from concourse.bass_utils import run_bass_kernel_spmd

F32 = mybir.dt.float32
BF16 = mybir.dt.bfloat16
AF = mybir.ActivationFunctionType
ALU = mybir.AluOpType
AX = mybir.AxisListType

NDS = 32
import os as _os0
SAME_ENG_WINDOW = int(_os0.environ.get("SEW", "6"))
MMDT = BF16
import os as _os
USE_POOL = False


class V:
    __slots__ = ("t", "ap")

    def __init__(self, t, ap):
        self.t = t
        self.ap = ap

    def __getitem__(self, idx):
        return V(self.t, self.ap[idx])

    def r(self, pat, **kw):
        return V(self.t, self.ap.rearrange(pat, **kw))

    def bc(self, shape):
        return V(self.t, self.ap.to_broadcast(list(shape)))

    def us(self, axis):
        return V(self.t, self.ap.unsqueeze(axis))

    def pb(self, n):
        return V(self.t, self.ap.partition_broadcast(n))


class Tl:
    __slots__ = ("h", "lw", "rd", "excl", "acc", "name")

    def __init__(self, h, name="", excl=False, acc=False):
        self.h = h
        self.lw = {}
        self.rd = {}
        self.excl = excl
        self.acc = acc
        self.name = name

    def __getitem__(self, idx):
        return V(self, self.h[idx])

    def v(self):
        return V(self, self.h)


class Ring:
    def __init__(self, tiles):
        self.tiles = tiles
        self.i = 0

    def next(self):
        t = self.tiles[self.i % len(self.tiles)]
        self.i += 1
        return t


class K:
    def __init__(self, nc, es):
        self.nc = nc
        self.stack = [es]
        self.E = dict(pe=nc.tensor, dve=nc.vector, act=nc.scalar, pool=nc.gpsimd, sp=nc.sync)
        self.sem = {e: es.enter_context(nc.semaphore("s_" + e)) for e in self.E}
        self.cnt = {e: 0 for e in self.E}
        self.waited = {}
        self.dsem = [es.enter_context(nc.semaphore("d%d" % i)) for i in range(NDS)]
        self.dcnt = [0] * NDS
        self.dnext = 0
        self.dnext_sw = 0
        self.nid = 0
        self.out_tickets = []
        self.PS = [Tl(es.enter_context(nc.psum_tensor("psb%d" % i, [128, 512], F32)), "psb%d" % i, excl=True)
                   for i in range(8)]
        self.psi = 0
        self.evi = 0

    @contextmanager
    def scope(self):
        es = ExitStack()
        self.stack.append(es)
        try:
            yield
        finally:
            self.barrier()
            self.stack.pop()
            es.close()

    def sb(self, shape, dt=F32, name=None):
        self.nid += 1
        name = (name or "t") + "_%d" % self.nid
        return Tl(self.stack[-1].enter_context(self.nc.sbuf_tensor(name, list(shape), dt)), name)

    def ring(self, n, shape, dt=F32, name=None):
        return Ring([self.sb(shape, dt, name) for _ in range(n)])

    def psn(self):
        t = self.PS[self.psi % 8]
        self.psi += 1
        return t

    def dram(self, name, shape, dt=F32):
        return Tl(self.nc.dram_tensor(name, list(shape), dt, kind="Internal").ap(), name, acc=True)

    def _wait(self, e, tk):
        sem, val, key, teng = tk
        if self.waited.get((e, key), 0) >= val:
            return
        self.E[e].wait_ge(sem, val)
        self.waited[(e, key)] = val

    def _need(self, e, tk):
        if tk[3] != e:
            return True
        if e == "pe":
            return False
        return self.cnt[e] - tk[1] < SAME_ENG_WINDOW

    def _deps(self, e, outs, ins):
        for t in ins:
            if t.excl:
                continue
            for tk in t.lw.values():
                if self._need(e, tk):
                    self._wait(e, tk)
        for t in list(outs) + [t for t in ins if t.excl]:
            if not t.acc:
                for tk in t.lw.values():
                    if self._need(e, tk):
                        self._wait(e, tk)
            for rk in t.rd.values():
                if self._need(e, rk):
                    self._wait(e, rk)

    def _mark(self, tk, outs, ins):
        for t in ins:
            if t.excl:
                continue
            t.rd[tk[2]] = tk
        for t in list(outs) + [t for t in ins if t.excl]:
            if t.acc:
                t.lw[tk[2]] = tk
            else:
                t.lw = {tk[2]: tk}
                t.rd = {}

    def op(self, e, fn, outs=(), ins=()):
        self._deps(e, outs, ins)
        ins_obj = fn(self.E[e])
        self.cnt[e] += 1
        ins_obj.then_inc(self.sem[e], 1)
        tk = (self.sem[e], self.cnt[e], e, e)
        self._mark(tk, outs, ins)
        return tk

    def dma(self, q, out, in_, final=False):
        outs = [out.t]
        ins = [in_.t]
        self._deps(q, outs, ins)
        if q == "pool":
            i = NDS - 8 + self.dnext_sw
            self.dnext_sw = (self.dnext_sw + 1) % 8
        else:
            i = self.dnext
            self.dnext = (self.dnext + 1) % (NDS - 8)
        if self.dcnt[i] > 0:
            self._wait(q, (self.dsem[i], 16 * self.dcnt[i], ("d", i), "dma"))
        ins_obj = self.E[q].dma_start(out=out.ap, in_=in_.ap)
        self.dcnt[i] += 1
        ins_obj.then_inc(self.dsem[i], 16)
        tk = (self.dsem[i], 16 * self.dcnt[i], ("d", i), "dma")
        self._mark(tk, outs, ins)
        if final:
            self.out_tickets.append(tk)
        return tk

    def barrier(self):
        for e in self.E:
            for x in self.E:
                if x != e and self.cnt[x] > 0:
                    self._wait(e, (self.sem[x], self.cnt[x], x, x))
            for i in range(NDS):
                if self.dcnt[i] > 0:
                    self._wait(e, (self.dsem[i], 16 * self.dcnt[i], ("d", i), "dma"))

    def finish(self):
        self.barrier()

    @staticmethod
    def _sv(x, ins):
        if isinstance(x, V):
            ins.append(x.t)
            return x.ap
        return x

    def act(self, out, in_, func, bias=None, scale=1.0):
        ins = [in_.t]
        kw = dict(out=out.ap, in_=in_.ap, func=func)
        if bias is not None:
            kw["bias"] = self._sv(bias, ins)
        kw["scale"] = self._sv(scale, ins)
        return self.op("act", lambda e: e.activation(**kw), outs=[out.t], ins=ins)

    def tt(self, out, a, b, op, eng="dve"):
        return self.op(eng, lambda e: e.tensor_tensor(out=out.ap, in0=a.ap, in1=b.ap, op=op),
                       outs=[out.t], ins=[a.t, b.t])

    def ts(self, out, a, s1, op0, s2=None, op1=None, eng="dve"):
        ins = [a.t]
        kw = dict(out=out.ap, in0=a.ap, scalar1=self._sv(s1, ins), scalar2=self._sv(s2, ins), op0=op0)
        if op1 is not None:
            kw["op1"] = op1
        return self.op(eng, lambda e: e.tensor_scalar(**kw), outs=[out.t], ins=ins)

    def stt(self, out, a, s, b, op0, op1, eng="dve"):
        ins = [a.t, b.t]
        sc = self._sv(s, ins)
        return self.op(eng, lambda e: e.scalar_tensor_tensor(out=out.ap, in0=a.ap, scalar=sc, in1=b.ap,
                                                              op0=op0, op1=op1), outs=[out.t], ins=ins)

    def copy(self, out, in_, eng="dve"):
        if eng == "act":
            return self.op("act", lambda e: e.activation(out=out.ap, in_=in_.ap, func=AF.Copy),
                           outs=[out.t], ins=[in_.t])
        return self.op(eng, lambda e: e.tensor_copy(out=out.ap, in_=in_.ap), outs=[out.t], ins=[in_.t])

    def evac(self, out, in_):
        self.evi += 1
        return self.copy(out, in_, "act" if self.evi % 2 else "dve")

    def recip(self, out, in_):
        return self.op("dve", lambda e: e.reciprocal(out=out.ap, in_=in_.ap), outs=[out.t], ins=[in_.t])

    def memset(self, out, val, eng="pool"):
        return self.op(eng, lambda e: e.memset(out.ap, val), outs=[out.t])

    def mm(self, out, lhsT, rhs, start=True, stop=True):
        return self.op("pe", lambda e: e.matmul(out.ap, lhsT=lhsT.ap, rhs=rhs.ap, start=start, stop=stop),
                       outs=[out.t], ins=[lhsT.t, rhs.t])

    def tr(self, out, in_, ident):
        return self.op("pe", lambda e: e.transpose(out.ap, in_.ap, ident.ap), outs=[out.t], ins=[in_.t, ident.t])

    def scan(self, out, d0, d1, init=0.0, op0=ALU.mult, op1=ALU.add):
        ins = [d0.t, d1.t]
        iv = self._sv(init, ins)
        return self.op("dve", lambda e: e.tensor_tensor_scan(out=out.ap, data0=d0.ap, data1=d1.ap, initial=iv,
                                                              op0=op0, op1=op1), outs=[out.t], ins=ins)

    def reduce(self, out, in_, op=ALU.add, axis=AX.X):
        return self.op("dve", lambda e: e.tensor_reduce(out=out.ap, in_=in_.ap, op=op, axis=axis),
                       outs=[out.t], ins=[in_.t])

    def aselect(self, out, in_, pattern, cmp, fill, base, cm):
        return self.op("pool", lambda e: e.affine_select(out=out.ap, in_=in_.ap, pattern=pattern, compare_op=cmp,
                                                         fill=fill, base=base, channel_multiplier=cm),
                       outs=[out.t], ins=[in_.t])


D = 1024
DFF = 4096
EPS = 1e-5
RW_GN_EPS = 64e-5
AB_IN = 4384
CD_IN = 2592

IN_SPECS = [
    ("norm_mix", (2, 1024)), ("norm_mlp", (2, 1024)), ("norm_final", (1024,)),
    ("mlp_w1", (2, 1024, 4096)), ("mlp_w2", (2, 4096, 1024)),
    ("ab_w_in", (1, 1024, 4384)), ("ab_w_out", (1, 1536, 1024)),
    ("rw_mu", (1, 2, 1792)), ("rw_w0", (1, 2, 512)), ("rw_w2", (1, 2, 64, 512)), ("rw_a0", (1, 512)),
    ("rw_a2", (1, 64, 512)), ("rw_g2", (1, 128, 512)), ("rw_k_k", (1, 512)), ("rw_k_a", (1, 512)),
    ("rw_r_k", (1, 8, 64)), ("rw_ln_w", (1, 512)),
    ("mb_conv_w", (1, 3, 1536)), ("mb_conv_b", (1, 1536)), ("mb_dt_bias", (1, 2, 16)),
    ("mb_A_log", (1, 2, 16)), ("mb_D", (1, 16)), ("mb_norm_w", (1, 1024)),
    ("cd_w_in", (1, 1024, 2592)), ("cd_w_out", (1, 1536, 1024)),
    ("s5_A_re", (1, 2, 32, 64)), ("s5_A_im", (1, 2, 32, 64)), ("s5_log_dt", (1, 2, 32)),
    ("s5_B_re", (1, 32, 64, 16)), ("s5_B_im", (1, 32, 64, 16)),
    ("s5_C_re", (1, 2, 32, 16, 64)), ("s5_C_im", (1, 2, 32, 16, 64)),
    ("s5_D", (1, 512)), ("s5_glu_w", (1, 512, 512)), ("s5_glu_b", (1, 512)),
    ("ml_conv_w", (1, 3, 1024)), ("ml_conv_b", (1, 1024)),
    ("ml_wq", (1, 256, 4, 4)), ("ml_wk", (1, 256, 4, 4)), ("ml_wv", (1, 256, 4, 4)),
    ("ml_i_b", (1, 2, 8)), ("ml_f_b", (1, 2, 8)), ("ml_norm_w", (1, 1024)), ("ml_skip", (1, 1024)),
]


class Ctx:
    pass


def col(ap1d, p=128):
    return ap1d.rearrange("(c p) -> p c", p=p)


def stage_consts(k, C):
    C.ident = k.sb([128, 128], name="ident")
    k.memset(C.ident[:], 1.0)
    k.aselect(C.ident[:], C.ident[:], [[-1, 128]], ALU.is_equal, 0.0, 0, 1)
    C.ones = k.sb([128, 128], name="ones")
    k.memset(C.ones[:], 1.0)
    C.bo = k.sb([128, 128], name="blockones")
    k.memset(C.bo[:], 0.0)
    k.memset(C.bo[0:64, 0:64], 1.0)
    k.memset(C.bo[64:128, 64:128], 1.0)
    C.cst = k.sb([128, 8], name="cst")
    for i, v in enumerate([0.0, 1.0, -0.5, EPS, RW_GN_EPS]):
        k.memset(C.cst[:, i:i + 1], v)
    C.mlow = k.sb([128, 128], name="mlow")
    k.memset(C.mlow[:], 1.0)
    k.aselect(C.mlow[:], C.mlow[:], [[1, 128]], ALU.is_ge, 0.0, 0, -1)
    C.mup = k.sb([128, 128], name="mup")
    k.memset(C.mup[:], 1.0)
    k.aselect(C.mup[:], C.mup[:], [[-1, 128]], ALU.is_ge, 0.0, 0, 1)
    C.sel = k.sb([32, 32, 128], name="sel")
    k.memset(C.sel[:], 1.0)
    k.aselect(C.sel[:], C.sel[:], [[-1, 32], [0, 128]], ALU.is_equal, 0.0, 0, 1)
    C.hm = k.sb([128, 2], name="halfmask")
    k.memset(C.hm[:], 0.0)
    k.memset(C.hm[0:64, 0:1], 1.0)
    k.memset(C.hm[64:128, 1:2], 1.0)


def stage_load_x(k, C):
    T = C.T
    xf = C.x.r("b s d -> (b s) d")
    with k.scope():
        xr = k.ring(2, [128, 4, 1024], name="xin")
        hr = k.ring(3, [128, 512], name="hout")
        for tb in range(T // 512):
            xt = xr.next()
            k.dma("sp", xt[:], xf[tb * 512:(tb + 1) * 512, :].r("(j p) d -> p j d", p=128))
            for c in range(8):
                ps = k.psn()
                for j in range(4):
                    k.tr(ps[:, j * 128:(j + 1) * 128], xt[:, j, c * 128:(c + 1) * 128], C.ident[:])
                ho = hr.next()
                k.evac(ho[:], ps[:])
                k.dma("pool", C.hT[c * 128:(c + 1) * 128, tb * 512:(tb + 1) * 512], ho[:])


def rmsnorm_block(k, C, wcol, tok0, ntok, xn, R):
    for t0 in range(0, ntok, 512):
        h = R["h"].next()
        k.dma("sp", h[:], C.hT[:, tok0 + t0:tok0 + t0 + 512].r("(c p) t -> p c t", p=128))
        sq = R["sq"].next()
        k.act(sq[:], h[:], AF.Square)
        ps = k.psn()
        for c in range(8):
            k.mm(ps[:], C.ones[:], sq[:, c, :], start=(c == 0), stop=(c == 7))
        rs = R["rs"].next()
        k.act(rs[:], ps[:], AF.Sqrt, bias=C.cst[:, 3:4], scale=1.0 / D)
        k.recip(rs[:], rs[:])
        for c in range(8):
            k.stt(xn[:, c, t0:t0 + 512], h[:, c, :], wcol[:, c:c + 1], rs[:], ALU.mult, ALU.mult)


def norm_rings(k):
    return dict(h=k.ring(2, [128, 8, 512], name="nh"), sq=k.ring(1, [128, 8, 512], name="nsq"),
                rs=k.ring(2, [128, 512], name="nrs"))


def dense(k, xT, KC, W, N, ntok, epi, wring, wgrp):
    for n0 in range(0, N, wgrp):
        nw = min(wgrp, N - n0)
        wt = wring.next()
        k.dma("pool", wt[:, :, 0:nw], W[:, n0:n0 + nw].r("(c p) n -> p c n", p=128))
        for m0 in range(0, nw, 128):
            mw = min(128, nw - m0)
            for t0 in range(0, ntok, 512):
                ps = k.psn()
                for c in range(KC):
                    k.mm(ps[0:mw, :], wt[:, c, m0:m0 + mw], xT[:, c, t0:t0 + 512], start=(c == 0), stop=(c == KC - 1))
                epi(n0 + m0, mw, t0, ps)


def stage_inproj(k, C, layer, W, N, nw_ap):
    T, TB = C.T, C.TB
    with k.scope():
        R = norm_rings(k)
        wcol = k.sb([128, 8], name="nw")
        k.dma("sp", wcol[:], nw_ap)
        xn = k.sb([128, 8, TB], MMDT, name="xn")
        wring = k.ring(2, [128, 8, 512], MMDT, name="win")
        oring = k.ring(3, [128, 512], name="cout")
        import os
        DBG = os.environ.get("KDBG", "")
        for tb0 in range(0, T, TB):
            if "nonorm" in DBG:
                k.memset(xn[:], 1.0)
            else:
                rmsnorm_block(k, C, wcol, tb0, TB, xn, R)

            def epi(n, mw, t0, ps):
                o = oring.next()
                k.evac(o[0:mw, :], ps[0:mw, :])
                if "nostore" not in DBG:
                    k.dma("sp", C.colsT[n:n + mw, tb0 + t0:tb0 + t0 + 512], o[0:mw, :])
            if "nodense" not in DBG:
                dense(k, xn[:], 8, W, (512 if "small" in DBG else N), TB, epi, wring, 512)


def stage_outproj(k, C, W):
    T, TB = C.T, C.TB
    with k.scope():
        xin = k.sb([128, 12, TB], MMDT, name="mixin")
        wring = k.ring(2, [128, 12, 512], MMDT, name="wout")
        hring = k.ring(3, [128, 512], name="hres")
        for tb0 in range(0, T, TB):
            k.dma("pool", xin[:], C.mixT[:, tb0:tb0 + TB].r("(c p) t -> p c t", p=128))

            def epi(n, mw, t0, ps):
                hr = hring.next()
                k.dma("sp", hr[:], C.hT[n:n + 128, tb0 + t0:tb0 + t0 + 512])
                k.tt(hr[:], hr[:], ps[:], ALU.add)
                k.dma("sp", C.hT[n:n + 128, tb0 + t0:tb0 + t0 + 512], hr[:])
            dense(k, xin[:], 12, W, 1024, TB, epi, wring, 512)


def stage_mlp(k, C, layer):
    T, TB = C.T, C.TB
    W1 = C.inp["mlp_w1"][layer]
    W2 = C.inp["mlp_w2"][layer]
    with k.scope():
        R = norm_rings(k)
        wcol = k.sb([128, 8], name="nw")
        k.dma("sp", wcol[:], C.inp["norm_mlp"][layer].r("(c p) -> p c", p=128))
        xn = k.sb([128, 8, TB], MMDT, name="xn")
        hid = k.sb([128, 32, TB], MMDT, name="hid")
        w1r = k.ring(2, [128, 8, 512], MMDT, name="w1")
        w2r = k.ring(2, [128, 32, 128], MMDT, name="w2")
        tring = k.ring(2, [128, 512], name="relu")
        hring = k.ring(3, [128, 512], name="hres")
        for tb0 in range(0, T, TB):
            rmsnorm_block(k, C, wcol, tb0, TB, xn, R)

            def epi1(n, mw, t0, ps):
                tmp = tring.next()
                k.act(tmp[:], ps[:], AF.Relu)
                k.tt(hid[:, n // 128, t0:t0 + 512], tmp[:], tmp[:], ALU.mult)
            dense(k, xn[:], 8, W1, DFF, TB, epi1, w1r, 512)

            def epi2(n, mw, t0, ps):
                hr = hring.next()
                k.dma("sp", hr[:], C.hT[n:n + 128, tb0 + t0:tb0 + t0 + 512])
                k.tt(hr[:], hr[:], ps[:], ALU.add)
                k.dma("sp", C.hT[n:n + 128, tb0 + t0:tb0 + t0 + 512], hr[:])
            dense(k, hid[:], 32, W2, D, TB, epi2, w2r, 128)


def stage_final(k, C):
    T = C.T
    of = C.out.r("b s d -> (b s) d")
    with k.scope():
        R = norm_rings(k)
        wcol = k.sb([128, 8], name="nw")
        k.dma("sp", wcol[:], C.inp["norm_final"].v().r("(c p) -> p c", p=128))
        xnr = k.ring(2, [128, 8, 512], name="xnf")
        orr = k.ring(2, [128, 4, 1024], name="outt")
        for t0 in range(0, T, 512):
            xn = xnr.next()
            rmsnorm_block(k, C, wcol, t0, 512, xn, R)
            ot = orr.next()
            for j in range(4):
                for c0 in range(0, 8, 4):
                    ps = k.psn()
                    for c in range(c0, c0 + 4):
                        k.tr(ps[:, (c - c0) * 128:(c - c0 + 1) * 128], xn[:, c, j * 128:(j + 1) * 128], C.ident[:])
                    k.evac(ot[:, j, c0 * 128:(c0 + 4) * 128], ps[:])
            k.dma("sp", of[t0:t0 + 512, :].r("(j p) d -> p j d", p=128), ot[:], final=True)


def to_tokmajor(k, C, src_fn, nchunk, ntile, dst_fn, oring):
    for j in range(ntile):
        for c0 in range(0, nchunk, 4):
            n = min(4, nchunk - c0)
            ps = k.psn()
            for c in range(c0, c0 + n):
                k.tr(ps[:, (c - c0) * 128:(c - c0 + 1) * 128], src_fn(c, j), C.ident[:])
            o = oring.next()
            k.evac(o[:, 0:n * 128], ps[:, 0:n * 128])
            k.dma("sp", dst_fn(j, c0, n), o[:, 0:n * 128])


def softplus_neg(k, out, in_, nbias, tmp):
    k.act(tmp, in_, AF.Exp, bias=nbias, scale=-1.0)
    k.act(out, tmp, AF.Ln, bias=1.0)


def stage_rwkv_prep(k, C):
    S, NB, T = C.S, C.NB, C.T
    BLK = min(512, S)
    I = C.inp
    sc = C.scr
    with k.scope():
        mu = k.sb([128, 14, 2], name="mu")
        for m_ in range(2):
            k.dma("sp", mu[:, :, m_], I["rw_mu"][0, m_].r("(c p) -> p c", p=128))
        cmu = k.sb([128, 14], name="cmu")
        k.tt(cmu[:], mu[:, :, 0], mu[:, :, 1], ALU.add)
        k.ts(cmu[:], cmu[:], -1.0, ALU.mult, 1.0, ALU.add)
        w0n = k.sb([128, 2, 4], name="w0n")
        for d_ in range(2):
            k.dma("sp", w0n[:, d_, :], I["rw_w0"][0, d_].r("(c p) -> p c", p=128))
        k.ts(w0n[:], w0n[:], -1.0, ALU.mult)
        w2 = k.sb([64, 2, 512], name="w2")
        k.dma("sp", w2[:], I["rw_w2"][0].r("d r c -> r d c"))
        a2 = k.sb([128, 512], name="a2")
        k.dma("sp", a2[64:128, :], I["rw_a2"][0])
        g2 = k.sb([128, 512], name="g2")
        k.dma("sp", g2[:], I["rw_g2"][0])
        pc = k.sb([128, 6, 4], name="pcols")
        for i, nm in enumerate(["rw_a0", "rw_k_k", "rw_k_a"]):
            k.dma("sp", pc[:, i, :], I[nm][0].r("(c p) -> p c", p=128))
        k.ts(pc[:, 3, :], pc[:, 2, :], -1.0, ALU.mult, 1.0, ALU.add)
        k.dma("sp", pc[:, 4, :], I["rw_r_k"][0].r("h k -> (h k)").r("(c p) -> p c", p=128))

        halo = k.ring(1, [128, 14, BLK + 2], name="halo")
        SHr = k.ring(1, [128, 14, BLK], name="sh")
        tmpr = k.ring(3, [128, BLK], name="rtmp")
        twr = k.ring(1, [128, BLK], name="rtw")
        sgr = k.ring(1, [128, BLK], name="rsg")
        outr = k.ring(4, [128, BLK], name="rout")
        Ar = k.ring(1, [128, 4, BLK], name="A")
        vecr = k.ring(1, [128, 7, 4, BLK], name="vecs")
        tokr = k.ring(3, [128, 512], name="tokout")
        for b in range(NB):
            for s0 in range(0, S, BLK):
                g0 = b * S + s0
                hl = halo.next()
                lo = 1 if s0 == 0 else 0
                hi = 1 if s0 + BLK == S else 0
                if lo:
                    k.memset(hl[:, :, 0:1], 0.0)
                if hi:
                    k.memset(hl[:, :, BLK + 1:BLK + 2], 0.0)
                k.dma("sp", hl[:, :, lo:BLK + 2 - hi],
                      C.colsT[0:1792, g0 - 1 + lo:g0 + BLK + 1 - hi].r("(c p) t -> p c t", p=128))
                SH = SHr.next()
                for c in range(14):
                    k.ts(SH[:, c, :], hl[:, c, 1:BLK + 1], cmu[:, c:c + 1], ALU.mult)
                    k.stt(SH[:, c, :], hl[:, c, 0:BLK], mu[:, c, 0:1], SH[:, c, :], ALU.mult, ALU.add)
                    k.stt(SH[:, c, :], hl[:, c, 2:BLK + 2], mu[:, c, 1:2], SH[:, c, :], ALU.mult, ALU.add)
                VE = vecr.next()
                tw = twr.next()
                k.act(tw[0:64, :], SH[0:64, 12, :], AF.Tanh)
                for d in range(2):
                    for cc in range(4):
                        ps = k.psn()
                        k.mm(ps[:, 0:BLK], w2[0:64, d, cc * 128:(cc + 1) * 128], tw[0:64, :])
                        t1 = tmpr.next()
                        softplus_neg(k, t1[:], ps[:, 0:BLK], w0n[:, d, cc:cc + 1], t1[:])
                        k.act(t1[:], t1[:], AF.Exp, bias=C.cst[:, 2:3], scale=-1.0)
                        k.ts(VE[:, 5 + d, cc, :], t1[:], -1.0, ALU.mult)
                A = Ar.next()
                sg = sgr.next()
                k.act(sg[:], SH[:, 13, :], AF.Sigmoid)
                for cc in range(4):
                    ps = k.psn()
                    k.mm(ps[:, 0:BLK], a2[64:128, cc * 128:(cc + 1) * 128], SH[64:128, 12, :])
                    k.act(A[:, cc, :], ps[:, 0:BLK], AF.Sigmoid, bias=pc[:, 0, cc:cc + 1])
                    ps = k.psn()
                    k.mm(ps[:, 0:BLK], g2[:, cc * 128:(cc + 1) * 128], sg[:])
                    o = outr.next()
                    k.evac(o[:], ps[:, 0:BLK])
                    k.dma("sp", sc["gT"][cc * 128:(cc + 1) * 128, g0:g0 + BLK], o[:])
                    kk = tmpr.next()
                    k.ts(kk[:], SH[:, 4 + cc, :], pc[:, 1, cc:cc + 1], ALU.mult)
                    sq = tmpr.next()
                    k.tt(sq[:], kk[:], kk[:], ALU.mult)
                    ps = k.psn()
                    k.mm(ps[:, 0:BLK], C.bo[:], sq[:])
                    k.ts(sq[:], ps[:, 0:BLK], 1e-12, ALU.max)
                    k.act(sq[:], sq[:], AF.Sqrt)
                    k.recip(sq[:], sq[:])
                    k.tt(kk[:], kk[:], sq[:], ALU.mult)
                    k.ts(VE[:, 4, cc, :], kk[:], -1.0, ALU.mult)
                    k.tt(VE[:, 1, cc, :], kk[:], A[:, cc, :], ALU.mult)
                    t2 = tmpr.next()
                    k.ts(t2[:], A[:, cc, :], pc[:, 2, cc:cc + 1], ALU.mult, pc[:, 3, cc:cc + 1], ALU.add)
                    k.tt(VE[:, 2, cc, :], SH[:, 4 + cc, :], t2[:], ALU.mult)
                    k.copy(VE[:, 0, cc, :], SH[:, cc, :], "pool")
                    k.copy(VE[:, 3, cc, :], SH[:, 8 + cc, :], "pool")
                    k.stt(t2[:], SH[:, cc, :], pc[:, 4, cc:cc + 1], VE[:, 2, cc, :], ALU.mult, ALU.mult)
                    ps = k.psn()
                    k.mm(ps[:, 0:BLK], C.bo[:], t2[:])
                    o = outr.next()
                    k.tt(o[:], ps[:, 0:BLK], SH[:, 8 + cc, :], ALU.mult)
                    k.dma("sp", sc["bonusT"][cc * 128:(cc + 1) * 128, g0:g0 + BLK], o[:])
                for vi, nm in enumerate(["rTok", "bTok", "kTok", "vTok", "aTok", "lw0Tok", "lw1Tok"]):
                    to_tokmajor(k, C, lambda c, j, vi=vi: VE[:, vi, c, j * 128:(j + 1) * 128], 4, BLK // 128,
                                lambda j, c0, n, nm=nm: sc[nm][g0 + j * 128:g0 + (j + 1) * 128, :], tokr)


def stage_rwkv_scan(k, C):
    S, NB, T = C.S, C.NB, C.T
    L = 64
    NCK = S // L
    sc = C.scr
    H8 = 8
    with k.scope():
        def mk(name, pattern, cm, cmp):
            m = k.sb([L, L], name=name)
            k.memset(m[:], 1.0)
            k.aselect(m[:], m[:], pattern, cmp, 0.0, 0, cm)
            return m
        UP = mk("mUP", [[1, L]], -1, ALU.is_gt)
        LO = mk("mLO", [[-1, L]], 1, ALU.is_gt)
        UPI = mk("mUPI", [[1, L]], -1, ALU.is_ge)
        LOI = mk("mLOI", [[-1, L]], 1, ALU.is_ge)
        I64 = C.ident[0:L, 0:L]
        bc8 = lambda m: (m[:] if isinstance(m, Tl) else m).us(1).bc([L, H8, L])
        ones2 = C.ones[0:L, 0:2]

        def v3(t):
            return t[:].r("p (h x) -> p h x", x=L)

        inr = k.ring(3, [L, 6, 512], name="cin")
        big = k.ring(12, [L, 512], name="cw")
        bgb = k.ring(44, [L, 512], BF16, name="cwb")
        vbr = k.ring(3, [L, 512], BF16, name="vb")
        I64b = k.sb([L, L], BF16, name="i64b")
        k.copy(I64b[:], I64)
        SIG = {}
        SIGB = {}
        for b in range(NB):
            for d in range(2):
                SIG[(b, d)] = k.sb([L, 512], name="sig")
                k.memset(SIG[(b, d)][:], 0.0)
                SIGB[(b, d)] = k.sb([L, 512], BF16, name="sigb")
                k.memset(SIGB[(b, d)][:], 0.0)
        pl_r = k.ring(3, [L, H8, 2], name="pl")

        def mm8(lhs_fn, rhs_fn, ps, nacc=1):
            for h in range(H8):
                for i in range(nacc):
                    k.mm(ps[0:L, h * L:(h + 1) * L], lhs_fn(h, i), rhs_fn(h, i), start=(i == 0), stop=(i == nacc - 1))

        hs = lambda t, h: t[:, h * L:(h + 1) * L]

        for ci in range(NCK):
            for b in range(NB):
                for d in range(2):
                    c = ci if d == 0 else NCK - 1 - ci
                    g0 = b * S + c * L
                    MS_st, MS_ts, MI_st = (UP, LO, UPI) if d == 0 else (LO, UP, LOI)
                    X = inr.next()
                    for i, nm in enumerate(["rTok", "kTok", "vTok", "aTok", "bTok", "lw%dTok" % d]):
                        k.dma("sp", X[:, i, :], sc[nm][g0:g0 + L, :])
                    r_, k_, v_, a_, b_, lw_ = [X[:, i, :] for i in range(6)]
                    vb = vbr.next()
                    k.dma("pool", vb[:], sc["vTok"][g0:g0 + L, :])
                    v_ = vb[:]
                    ps_c = k.psn()
                    k.mm(ps_c[0:L, :], MI_st[:], lw_)
                    ps_r = k.psn()
                    k.mm(ps_r[0:L, :], MS_ts[:], lw_)
                    En, Ep, Er, Epv = big.next(), big.next(), big.next(), big.next()
                    k.act(En[:], ps_c[0:L, :], AF.Exp, scale=-1.0)
                    k.act(Ep[:], ps_c[0:L, :], AF.Exp)
                    k.tt(Epv[:], ps_c[0:L, :], lw_, ALU.subtract)
                    k.act(Epv[:], Epv[:], AF.Exp)
                    k.act(Er[:], ps_r[0:L, :], AF.Exp)
                    ps_p = k.psn()
                    for h in range(H8):
                        k.mm(ps_p[0:L, 2 * h:2 * h + 2], X[:, 5, h * L:(h + 1) * L], ones2)
                    PL = pl_r.next()
                    k.act(PL[:].r("p h x -> p (h x)"), ps_p[0:L, 0:2 * H8], AF.Exp)
                    at, bt, kt, rt = [big.next() for _ in range(4)]
                    Bh, Kh, atb = bgb.next(), bgb.next(), bgb.next()
                    k.tt(at[:], a_, Epv[:], ALU.mult)
                    k.tt(atb[:], a_, Epv[:], ALU.mult)
                    k.tt(bt[:], b_, En[:], ALU.mult)
                    k.tt(kt[:], k_, En[:], ALU.mult)
                    k.tt(rt[:], r_, Ep[:], ALU.mult)
                    k.tt(Bh[:], b_, Er[:], ALU.mult)
                    k.tt(Kh[:], k_, Er[:], ALU.mult)
                    FT = []
                    for src in (at, bt, kt, rt):
                        ps = k.psn()
                        for h in range(H8):
                            k.tr(ps[0:L, h * L:(h + 1) * L], hs(src, h), I64)
                        o = bgb.next()
                        k.evac(o[:], ps[0:L, :])
                        FT.append(o)
                    AT, BT, KT, RT = FT
                    def prod(lt, rt_, mask):
                        ps = k.psn()
                        mm8(lambda h, i: hs(lt, h), lambda h, i: hs(rt_, h), ps)
                        o = bgb.next()
                        k.tt(v3(o), ps[0:L, :].r("p (h x) -> p h x", x=L), bc8(mask), ALU.mult)
                        return o
                    Nn = prod(AT, BT, MS_ts)
                    Mm = prod(BT, AT, MS_st)
                    AakT = prod(KT, AT, MS_st)
                    ArbT = prod(BT, RT, MI_st)
                    ArkT = prod(KT, RT, MI_st)
                    Xt = bgb.next()
                    k.tt(v3(Xt), v3(Mm), I64b[:].us(1).bc([L, H8, L]), ALU.add)
                    Mp, Np = Mm, Nn
                    for lev in range(5):
                        ps_n = k.psn()
                        mm8(lambda h, i: hs(Mp, h), lambda h, i: hs(Np, h), ps_n)
                        Np2 = bgb.next()
                        k.evac(Np2[:], ps_n[0:L, :])
                        if lev < 4:
                            ps_m = k.psn()
                            mm8(lambda h, i: hs(Np, h), lambda h, i: hs(Mp, h), ps_m)
                            Mp2 = bgb.next()
                            k.evac(Mp2[:], ps_m[0:L, :])
                        else:
                            Mp2 = None
                        ps_x = k.psn()
                        mm8(lambda h, i: hs(Np2, h), lambda h, i: hs(Xt, h), ps_x)
                        Xn = bgb.next()
                        k.tt(Xn[:], ps_x[0:L, :], Xt[:], ALU.add)
                        Xt, Mp, Np = Xn, Mp2, Np2
                    ps = k.psn()
                    mm8(lambda h, i: hs(Xt, h), lambda h, i: hs(atb, h), ps)
                    TA = bgb.next()
                    k.evac(TA[:], ps[0:L, :])
                    ps = k.psn()
                    mm8(lambda h, i: hs(AakT, h), lambda h, i: v_[:, h * L:(h + 1) * L], ps)
                    W2 = bgb.next()
                    k.evac(W2[:], ps[0:L, :])
                    ps = k.psn()
                    mm8(lambda h, i: hs(Xt, h), lambda h, i: hs(W2, h), ps)
                    TAV = bgb.next()
                    k.evac(TAV[:], ps[0:L, :])
                    ps = k.psn()
                    mm8(lambda h, i: hs(TA, h), lambda h, i: hs(ArbT, h), ps)
                    QhT = bgb.next()
                    k.tt(QhT[:], ps[0:L, :], RT[:], ALU.add)
                    ps = k.psn()
                    mm8(lambda h, i: hs(ArbT if i == 0 else ArkT, h),
                        lambda h, i: (hs(TAV, h) if i == 0 else v_[:, h * L:(h + 1) * L]), ps, nacc=2)
                    Yloc = big.next()
                    k.evac(Yloc[:], ps[0:L, :])
                    ps = k.psn()
                    mm8(lambda h, i: hs(TA, h), lambda h, i: hs(Bh, h), ps)
                    Gd = big.next()
                    k.tt(v3(Gd), I64.us(1).bc([L, H8, L]), PL[:, :, 0].us(2).bc([L, H8, L]), ALU.mult)
                    GT = bgb.next()
                    k.tt(GT[:], ps[0:L, :], Gd[:], ALU.add)
                    ps = k.psn()
                    mm8(lambda h, i: hs(Bh if i == 0 else Kh, h),
                        lambda h, i: (hs(TAV, h) if i == 0 else v_[:, h * L:(h + 1) * L]), ps, nacc=2)
                    Hh = big.next()
                    k.evac(Hh[:], ps[0:L, :])
                    Sg = SIG[(b, d)]
                    Sgb = SIGB[(b, d)]
                    ps = k.psn()
                    mm8(lambda h, i: hs(QhT, h), lambda h, i: hs(Sgb, h), ps)
                    Y = big.next()
                    k.tt(Y[:], ps[0:L, :], Yloc[:], ALU.add)
                    k.dma("pool", sc["ypTok"][d * T + g0:d * T + g0 + L, :], Y[:])
                    ps = k.psn()
                    mm8(lambda h, i: hs(GT, h), lambda h, i: hs(Sgb, h), ps)
                    k.tt(Sg[:], ps[0:L, :], Hh[:], ALU.add)
                    k.copy(Sgb[:], Sg[:], "act")


def stage_rwkv_post(k, C):
    S, NB, T = C.S, C.NB, C.T
    sc = C.scr
    I = C.inp
    with k.scope():
        lnw = k.sb([128, 4], name="lnw")
        k.dma("sp", lnw[:], I["rw_ln_w"][0].r("(c p) -> p c", p=128))
        inr = k.ring(2, [128, 8, 512], name="pin")
        wr = k.ring(4, [128, 512], name="pw")
        sr = k.ring(4, [128, 8], name="ps8")
        ynr = k.ring(2, [128, 4, 512], name="yn")
        fr = k.ring(3, [128, 512], name="pf")
        for t0 in range(0, T, 512):
            yn = ynr.next()
            for j in range(4):
                g0 = t0 + j * 128
                X = inr.next()
                srcs = [sc["ypTok"][g0:g0 + 128, :], sc["ypTok"][T + g0:T + g0 + 128, :]]
                for i, s_ in enumerate(srcs):
                    k.dma("sp", X[:, i, :], s_)
                y = wr.next()
                k.tt(y[:], X[:, 0, :], X[:, 1, :], ALU.add)
                pr = wr.next()
                mean = sr.next()
                k.reduce(mean[:], y[:].r("p (h v) -> p h v", v=64))
                k.ts(mean[:], mean[:], 1.0 / 64, ALU.mult)
                k.tt(y[:].r("p (h v) -> p h v", v=64), y[:].r("p (h v) -> p h v", v=64),
                     mean[:].us(2).bc([128, 8, 64]), ALU.subtract)
                k.tt(pr[:], y[:], y[:], ALU.mult)
                var = sr.next()
                k.reduce(var[:], pr[:].r("p (h v) -> p h v", v=64))
                k.act(var[:], var[:], AF.Sqrt, bias=C.cst[:, 4:5], scale=1.0 / 64)
                k.recip(var[:], var[:])
                k.tt(yn[:, j, :].r("p (h v) -> p h v", v=64), y[:].r("p (h v) -> p h v", v=64),
                     var[:].us(2).bc([128, 8, 64]), ALU.mult)
            for cc in range(4):
                ps = k.psn()
                for j in range(4):
                    k.tr(ps[:, j * 128:(j + 1) * 128], yn[:, j, cc * 128:(cc + 1) * 128], C.ident[:])
                bon = fr.next()
                k.dma("sp", bon[:], sc["bonusT"][cc * 128:(cc + 1) * 128, t0:t0 + 512])
                gt = fr.next()
                k.dma("sp", gt[:], sc["gT"][cc * 128:(cc + 1) * 128, t0:t0 + 512])
                k.stt(bon[:], ps[:], lnw[:, cc:cc + 1], bon[:], ALU.mult, ALU.add)
                k.tt(bon[:], bon[:], gt[:], ALU.mult)
                k.dma("sp", C.mixT[cc * 128:(cc + 1) * 128, t0:t0 + 512], bon[:])


def dwconv_silu(k, C, src, row0, nch, wcv, bcv, dst, drow0, halo, outr):
    S, NB = C.S, C.NB
    for b in range(NB):
        for c in range(nch):
            hl = halo.next()
            k.memset(hl[:, 0:1], 0.0)
            k.memset(hl[:, S + 1:S + 2], 0.0)
            k.dma("sp", hl[:, 1:S + 1], src[row0 + c * 128:row0 + (c + 1) * 128, b * S:(b + 1) * S])
            o = outr.next()
            k.ts(o[:], hl[:, 0:S], wcv[:, c, 0:1], ALU.mult)
            k.stt(o[:], hl[:, 1:S + 1], wcv[:, c, 1:2], o[:], ALU.mult, ALU.add)
            k.stt(o[:], hl[:, 2:S + 2], wcv[:, c, 2:3], o[:], ALU.mult, ALU.add)
            k.act(o[:], o[:], AF.Silu, bias=bcv[:, c:c + 1])
            k.dma("sp", dst[drow0 + c * 128:drow0 + (c + 1) * 128, b * S:(b + 1) * S], o[:])


def gate_cumsums(k, C, la, nrow, half, X, tmp, ones_row, m01):
    S, NB = C.S, C.NB
    for b in range(NB):
        sl = slice(b * S, (b + 1) * S)
        k.scan(X[0:nrow, sl], ones_row[0:nrow, 0:S], la[0:nrow, sl])
        k.scan(V(tmp.t, tmp.ap[0:nrow, sl][:, ::-1]), ones_row[0:nrow, 0:S], V(la.t, la.ap[0:nrow, sl][:, ::-1]))
    k.ts(X[0:nrow, :], X[0:nrow, :], m01[0:nrow, 0:1], ALU.mult)
    k.stt(X[0:nrow, :], tmp[0:nrow, :], m01[0:nrow, 1:2], X[0:nrow, :], ALU.mult, ALU.add)


def rowmask(k, nrow, half, name):
    m = k.sb([nrow, 2], name=name)
    k.memset(m[:], 1.0)
    k.aselect(m[:, 0:1], m[:, 0:1], [[0, 1]], ALU.is_gt, 0.0, half, -1)
    k.aselect(m[:, 1:2], m[:, 1:2], [[0, 1]], ALU.is_ge, 0.0, -half, 1)
    return m


def decay_attention(k, C, nhead, hpg, kT_fn, qT_fn, v_fn, vw, X, bsT, nxT, lmT, rowof, mode, out_fn, mdt, nst=2, nxb=None):
    xd = C.scr["xrow"]
    if X is not None:
        k.dma("sp", xd[0:X.ap.shape[0], 0:C.S], X)
    S = C.S
    NI = S // 128
    NW = 4
    WW = NW * 128
    with k.scope():
        STr = k.ring(nst, [128, NI, WW], BF16, name="STs")
        XBr = k.ring(nxb or nst, [128, 2 * hpg, WW], name="Xb")
        Er = k.ring(3, [128, WW], BF16, name="E")
        Tr = k.ring(3, [128, 128], name="Tdiag")
        Edr = k.ring(3, [128, 128], BF16, name="Ed")
        Mdr = k.ring(3, [128, 128], mdt, name="Md")
        STmr = k.ring(nst, [128, NW, 2, 128], BF16, name="STm")
        Mr = k.ring(3, [128, WW], mdt, name="M")
        for g in range(nhead // hpg):
            for TJ in range(NI // NW):
                t0 = TJ * WW
                Js = [TJ * NW + jj for jj in range(NW)]
                STs = STr.next()
                for I in range(NI):
                    ps = k.psn()
                    k.mm(ps[:, 0:WW], kT_fn(g, I), qT_fn(g, t0, WW))
                    k.evac(STs[:, I, :], ps[:, 0:WW])
                XB = XBr.next()
                for ii in range(2 * hpg):
                    d, hl = ii // hpg, ii % hpg
                    row = rowof(d, g * hpg + hl)
                    k.dma("sp", XB[:, ii, :], xd[row, t0:t0 + WW].pb(128))
                STm = STmr.next()
                for jj, J in enumerate(Js):
                    for d in range(2):
                        k.tt(STm[:, jj, d, :], STs[:, J, jj * 128:(jj + 1) * 128], (C.mlow if d == 0 else C.mup)[:],
                             ALU.mult)
                for hl in range(hpg):
                    h = g * hpg + hl
                    for dset in ([(0, 1)] if mode in ("sum", "sumT") else [(0,), (1,)]):
                        items = []
                        touch = {J: 0 for J in Js}
                        for d in dset:
                            for I in range(NI):
                                pure = [jj for jj, J in enumerate(Js) if (d == 0 and I < J) or (d == 1 and I > J)]
                                dg = [jj for jj, J in enumerate(Js) if I == J]
                                if not pure and not dg:
                                    continue
                                items.append((d, I, pure, dg[0] if dg else None))
                                for jj in pure + dg:
                                    touch[Js[jj]] += 1
                        if mode == "sumT":
                            accT = k.psn()
                            po = (h % 2) * 64
                        else:
                            accs = {J: k.psn() for J in Js}
                        seen = {J: 0 for J in Js}
                        nmm = 0
                        tot_mm = sum((1 if it[2] else 0) + (1 if it[3] is not None else 0) for it in items)
                        for (d, I, pure, dg) in items:
                            row = rowof(d, h)
                            xb = XB[:, d * hpg + hl, :]
                            if pure:
                                c0, c1 = pure[0] * 128, (pure[-1] + 1) * 128
                                E = Er.next()
                                M = Mr.next()
                                k.act(E[:, c0:c1], xb[:, c0:c1], AF.Exp, bias=bsT[:, I, row:row + 1])
                                k.tt(M[:, c0:c1], E[:, c0:c1], STs[:, I, c0:c1], ALU.mult)
                            if dg is not None:
                                cs = slice(dg * 128, (dg + 1) * 128)
                                Tm = Tr.next()
                                Ed = Edr.next()
                                Md = Mdr.next()
                                k.ts(Tm[:], xb[:, cs], nxT[:, I, row:row + 1], ALU.add, 0.0, ALU.min)
                                k.act(Ed[:], Tm[:], AF.Exp, bias=lmT[:, I, row:row + 1])
                                k.tt(Md[:], Ed[:], STm[:, dg, d, :], ALU.mult)
                            if mode == "sumT":
                                if pure:
                                    k.mm(accT[po:po + vw, c0:c1], v_fn(h, I), M[:, c0:c1],
                                         start=(nmm == 0), stop=(nmm == tot_mm - 1))
                                    nmm += 1
                                if dg is not None:
                                    k.mm(accT[po:po + vw, cs], v_fn(h, I), Md[:],
                                         start=(nmm == 0), stop=(nmm == tot_mm - 1))
                                    nmm += 1
                                continue
                            for jj in pure:
                                J = Js[jj]
                                k.mm(accs[J][:, 0:vw], M[:, jj * 128:(jj + 1) * 128], v_fn(h, I),
                                     start=(seen[J] == 0), stop=(seen[J] == touch[J] - 1))
                                seen[J] += 1
                            if dg is not None:
                                J = Js[dg]
                                k.mm(accs[J][:, 0:vw], Md[:], v_fn(h, I),
                                     start=(seen[J] == 0), stop=(seen[J] == touch[J] - 1))
                                seen[J] += 1
                        if mode == "sumT":
                            out_fn(h, TJ, 0, accT)
                        else:
                            for J in Js:
                                out_fn(h, J, dset[0], accs[J])


def stage_mamba(k, C):
    S, NB, T = C.S, C.NB, C.T
    I = C.inp
    sc = C.scr
    NI = S // 128
    with k.scope():
        wcv = k.sb([128, 12, 3], name="wcv")
        for kk_ in range(3):
            k.dma("sp", wcv[:, :, kk_], I["mb_conv_w"][0, kk_].r("(c p) -> p c", p=128))
        bcv = k.sb([128, 12], name="bcv")
        k.dma("sp", bcv[:], I["mb_conv_b"][0].r("(c p) -> p c", p=128))
        halo = k.ring(2, [128, S + 2], name="chalo")
        outr = k.ring(2, [128, S], name="cout")
        dwconv_silu(k, C, C.colsT, 2816, 12, wcv[:], bcv[:], sc["xbcT"], 0, halo, outr)
    import os
    DBG = os.environ.get("KDBG", "")
    if "mb1" in DBG:
        return
    with k.scope():
        dtb = k.sb([32, 1], name="dtb")
        k.dma("sp", dtb[:], I["mb_dt_bias"][0].r("d h -> (d h)").us(1))
        k.ts(dtb[:], dtb[:], -1.0, ALU.mult)
        Aneg = k.sb([32, 1], name="Aneg")
        k.dma("sp", Aneg[:], I["mb_A_log"][0].r("d h -> (d h)").us(1))
        k.act(Aneg[:], Aneg[:], AF.Exp)
        k.ts(Aneg[:], Aneg[:], -1.0, ALU.mult)
        m01 = rowmask(k, 32, 16, "m01")
        bsT = k.sb([128, NI, 32], name="bsT")
        nxT = k.sb([128, NI, 32], name="nxT")
        lmT = k.sb([128, NI, 32], name="lmT")
        xs_tok = k.sb([128, NI, 1024], BF16, name="xs_tok")
        BC = k.sb([128, 4, S], BF16, name="BC")
        xsr = k.ring(2, [128, 8, 128], name="xsr")
        ytr = k.ring(3, [128, 512], name="yT")
        xtr = k.ring(3, [128, 512], name="xT")
        Dcol = k.sb([128, 8], name="Dcol")
        for h_ in range(16):
            k.dma("sp", Dcol[(h_ % 2) * 64:(h_ % 2) * 64 + 64, h_ // 2:h_ // 2 + 1], I["mb_D"][0][h_:h_ + 1].pb(64))
        for b in range(NB):
            bs0 = b * S
            with k.scope():
                onesr = k.sb([32, S], name="onesr")
                k.memset(onesr[:], 1.0)
                dt = k.sb([32, S], name="dt")
                tmp = k.sb([32, S], name="gtmp")
                la = k.sb([32, S], name="la")
                X = k.sb([32, S], name="X")
                k.dma("sp", dt[:], C.colsT[4352:4384, bs0:bs0 + S])
                k.ts(dt[:], dt[:], dtb[:, 0:1], ALU.subtract)
                k.ts(tmp[:], dt[:], -1.0, ALU.mult)
                k.tt(tmp[:], tmp[:], dt[:], ALU.min)
                k.act(tmp[:], tmp[:], AF.Exp)
                k.act(tmp[:], tmp[:], AF.Ln, bias=1.0)
                k.stt(dt[:], dt[:], 0.0, tmp[:], ALU.max, ALU.add)
                k.ts(la[:], dt[:], Aneg[:, 0:1], ALU.mult)
                k.act(dt[:], dt[:], AF.Ln)
                k.scan(X[:], onesr[:], la[:])
                k.scan(V(tmp, tmp.h[:, ::-1]), onesr[:], V(la, la.h[:, ::-1]))
                k.ts(X[:], X[:], m01[:, 0:1], ALU.mult)
                k.stt(X[:], tmp[:], m01[:, 1:2], X[:], ALU.mult, ALU.add)
                for src, dst in ((dt, lmT), (X, nxT)):
                    for j0 in range(0, NI, 16):
                        ps = k.psn()
                        n = min(16, NI - j0)
                        for j in range(j0, j0 + n):
                            k.tr(ps[:, (j - j0) * 32:(j - j0 + 1) * 32], src[0:32, j * 128:(j + 1) * 128], C.ident[0:32, 0:32])
                        k.evac(dst[:, j0:j0 + n, :], ps[:, 0:n * 32].r("p (j r) -> p j r", r=32))

                k.dma("sp", sc["xrow"][0:32, 0:S], X[:])
            k.ts(nxT[:], nxT[:], -1.0, ALU.mult)
            k.tt(bsT[:], lmT[:], nxT[:], ALU.add)
            for j in range(NI):
                xin = xsr.next()
                k.dma("sp", xin[:], sc["xbcT"][0:1024, bs0 + j * 128:bs0 + (j + 1) * 128].r("(c p) t -> p c t", p=128))
                for c0 in (0, 4):
                    ps = k.psn()
                    for c in range(c0, c0 + 4):
                        k.tr(ps[:, (c - c0) * 128:(c - c0 + 1) * 128], xin[:, c, :], C.ident[:])
                    k.evac(xs_tok[:, j, c0 * 128:(c0 + 4) * 128], ps[:])
            k.dma("pool", BC[:], sc["xbcT"][1024:1536, bs0:bs0 + S].r("(c p) t -> p c t", p=128))
            ystate = {}
            if "mb2" in DBG:
                continue

            def out_fn(h, TJ, d, acc):
                c = h // 2
                po = (h % 2) * 64
                t0 = bs0 + TJ * 512
                if h % 2 == 0:
                    ystate["y"] = ytr.next()
                    ystate["x"] = xtr.next()
                    k.dma("sp", ystate["x"][:], sc["xbcT"][c * 128:(c + 1) * 128, t0:t0 + 512])
                y, xT = ystate["y"], ystate["x"]
                k.stt(y[po:po + 64, :], xT[po:po + 64, :], Dcol[po:po + 64, c:c + 1], acc[po:po + 64, 0:512],
                      ALU.mult, ALU.add)
                if h % 2 == 1:
                    k.dma("pool", sc["ymbT"][c * 128:(c + 1) * 128, t0:t0 + 512], y[:])

            decay_attention(
                k, C, 16, 8,
                lambda g, Ib: BC[:, g, Ib * 128:(Ib + 1) * 128],
                lambda g, t0, n: BC[:, 2 + g, t0:t0 + n],
                lambda h, Ib: xs_tok[:, Ib, h * 64:(h + 1) * 64], 64,
                None, bsT, nxT, lmT, lambda d, h: d * 16 + h, "sumT", out_fn, BF16)
    with k.scope():
        nw = k.sb([128, 8], name="mbnw")
        k.dma("sp", nw[:], I["mb_norm_w"][0].r("(c p) -> p c", p=128))
        yr = k.ring(2, [128, 8, 512], name="ytk")
        zr = k.ring(2, [128, 8, 512], name="z")
        gr = k.ring(2, [128, 8, 512], name="gy")
        sqr = k.ring(3, [128, 512], name="gsq")
        for t0 in range(0, T, 512):
            yt = yr.next()
            k.dma("sp", yt[:], sc["ymbT"][:, t0:t0 + 512].r("(c p) t -> p c t", p=128))
            z = zr.next()
            k.dma("sp", z[:], C.colsT[1792:2816, t0:t0 + 512].r("(c p) t -> p c t", p=128))
            k.act(z[:], z[:], AF.Silu)
            gy = gr.next()
            for c in range(8):
                k.tt(gy[:, c, :], yt[:, c, :], z[:, c, :], ALU.mult)
            for g in range(2):
                ps = k.psn()
                for c in range(4):
                    sq = sqr.next()
                    k.tt(sq[:], gy[:, g * 4 + c, :], gy[:, g * 4 + c, :], ALU.mult)
                    k.mm(ps[:], C.ones[:], sq[:], start=(c == 0), stop=(c == 3))
                rs = sqr.next()
                k.act(rs[:], ps[:], AF.Sqrt, bias=C.cst[:, 3:4], scale=1.0 / 512)
                k.recip(rs[:], rs[:])
                for c in range(4):
                    cc = g * 4 + c
                    k.stt(gy[:, cc, :], gy[:, cc, :], nw[:, cc:cc + 1], rs[:], ALU.mult, ALU.mult)
            k.dma("sp", C.mixT[512:1536, t0:t0 + 512].r("(c p) t -> p c t", p=128), gy[:])


def stage_s5(k, C):
    S, NB, T = C.S, C.NB, C.T
    I = C.inp
    sc = C.scr
    with k.scope():
        P = {}
        for nm in ("ar", "ai", "ldt"):
            P[nm] = k.sb([128, 32], name="s5" + nm)
        with k.nc.allow_non_contiguous_dma(reason="tiny param transposes"):
            for d in range(2):
                for gg in range(2):
                    k.dma("sp", P["ar"][gg * 64:(gg + 1) * 64, d * 16:(d + 1) * 16],
                          I["s5_A_re"][0, d].r("(j gg) p -> gg p j", gg=2)[gg])
                    k.dma("sp", P["ai"][gg * 64:(gg + 1) * 64, d * 16:(d + 1) * 16],
                          I["s5_A_im"][0, d].r("(j gg) p -> gg p j", gg=2)[gg])
                    k.dma("sp", P["ldt"][gg * 64:(gg + 1) * 64, d * 16:(d + 1) * 16],
                          I["s5_log_dt"][0, d].r("(j gg) -> gg j", gg=2)[gg].pb(64))
        sm = lambda nm: k.sb([128, 32], name="s5" + nm)
        dtt, th, mag, c_, s_, t1, t2 = [sm(n) for n in ("dt", "th", "mag", "c", "s", "t1", "t2")]
        k.act(dtt[:], P["ldt"][:], AF.Exp)
        k.ts(P["ar"][:], P["ar"][:], -1e-4, ALU.min)
        k.tt(th[:], dtt[:], P["ai"][:], ALU.mult)
        k.tt(mag[:], dtt[:], P["ar"][:], ALU.mult)
        k.act(mag[:], mag[:], AF.Exp)
        k.act(s_[:], th[:], AF.Sin, scale=1.0 / 16)
        k.act(t1[:], th[:], AF.Sin, scale=1.0 / 32)
        k.tt(t1[:], t1[:], t1[:], ALU.mult)
        k.ts(c_[:], t1[:], -2.0, ALU.mult, 1.0, ALU.add)
        for _ in range(4):
            k.tt(t1[:], c_[:], c_[:], ALU.mult)
            k.tt(t2[:], s_[:], s_[:], ALU.mult)
            k.tt(s_[:], s_[:], c_[:], ALU.mult)
            k.ts(s_[:], s_[:], 2.0, ALU.mult)
            k.tt(c_[:], t1[:], t2[:], ALU.subtract)
        NLEV = max(1, int(math.log2(S)))
        PW = k.sb([128, NLEV + 1, 2, 32], name="s5pw")
        k.copy(PW[:, 0, 0, :], c_[:])
        k.copy(PW[:, 0, 1, :], s_[:])
        for lv in range(1, NLEV + 1):
            k.tt(t1[:], PW[:, lv - 1, 0, :], PW[:, lv - 1, 0, :], ALU.mult)
            k.tt(t2[:], PW[:, lv - 1, 1, :], PW[:, lv - 1, 1, :], ALU.mult)
            k.tt(PW[:, lv, 0, :], t1[:], t2[:], ALU.subtract)
            k.tt(t1[:], PW[:, lv - 1, 0, :], PW[:, lv - 1, 1, :], ALU.mult)
            k.ts(PW[:, lv, 1, :], t1[:], 2.0, ALU.mult)
        abr, abi, den, fr_, fi_ = [sm(n) for n in ("abr", "abi", "den", "fr", "fi")]
        k.tt(abr[:], mag[:], c_[:], ALU.mult)
        k.tt(abi[:], mag[:], s_[:], ALU.mult)
        k.tt(den[:], P["ar"][:], P["ar"][:], ALU.mult)
        k.tt(t1[:], P["ai"][:], P["ai"][:], ALU.mult)
        k.tt(den[:], den[:], t1[:], ALU.add)
        k.recip(den[:], den[:])
        k.ts(t1[:], abr[:], -1.0, ALU.add)
        k.tt(fr_[:], t1[:], P["ar"][:], ALU.mult)
        k.tt(t2[:], abi[:], P["ai"][:], ALU.mult)
        k.tt(fr_[:], fr_[:], t2[:], ALU.add)
        k.tt(fr_[:], fr_[:], den[:], ALU.mult)
        k.tt(fi_[:], abi[:], P["ar"][:], ALU.mult)
        k.tt(t2[:], t1[:], P["ai"][:], ALU.mult)
        k.tt(fi_[:], fi_[:], t2[:], ALU.subtract)
        k.tt(fi_[:], fi_[:], den[:], ALU.mult)
        Bre = k.sb([128, 16, 16], name="Bre")
        Bim = k.sb([128, 16, 16], name="Bim")
        for gg in range(2):
            k.dma("sp", Bre[gg * 64:(gg + 1) * 64], I["s5_B_re"][0].r("(j gg) p m -> gg p j m", gg=2)[gg])
            k.dma("sp", Bim[gg * 64:(gg + 1) * 64], I["s5_B_im"][0].r("(j gg) p m -> gg p j m", gg=2)[gg])
        BT = k.sb([128, 2, 2, 16, 128], name="BT")
        bbr = k.ring(2, [128, 16], name="bb")
        BDr = k.ring(2, [128, 4, 2, 16], name="BD")
        for d in range(2):
            for ri in range(2):
                for j in range(16):
                    q = j % 4
                    cj = d * 16 + j
                    BD = BDr.next()
                    k.memset(BD[:], 0.0)
                    bb = bbr.next()
                    if ri == 0:
                        k.ts(bb[:], Bim[:, j, :], fi_[:, cj:cj + 1], ALU.mult)
                        k.stt(bb[:], Bre[:, j, :], fr_[:, cj:cj + 1], bb[:], ALU.mult, ALU.subtract)
                    else:
                        k.ts(bb[:], Bre[:, j, :], fi_[:, cj:cj + 1], ALU.mult)
                        k.stt(bb[:], Bim[:, j, :], fr_[:, cj:cj + 1], bb[:], ALU.mult, ALU.add)
                    for gg in range(2):
                        k.ts(BD[:, q, gg, :], bb[:], C.hm[:, gg:gg + 1], ALU.mult)
                    ps = k.psn()
                    k.tr(ps[:, 0:128], BD[:].r("p q g m -> p (q g m)"), C.ident[:])
                    k.evac(BT[:, d, ri, j, :], ps[:, 0:128])
        CT = k.sb([128, 2, 2, 4, 8, 128], BF16, name="CTz")
        k.memset(CT[:], 0.0)
        cinr = k.ring(2, [128, 128], name="cin")
        ctr = k.ring(2, [128, 128], name="ctf")
        for d in range(2):
            for ri, nm in enumerate(("s5_C_re", "s5_C_im")):
                for c4 in range(4):
                    cin = cinr.next()
                    src = I[nm][0, d].r("g m p -> (g m) p")[c4 * 128:(c4 + 1) * 128, :]
                    k.dma("sp", cin[:, 0:64], src)
                    k.dma("sp", cin[:, 64:128], src)
                    ps = k.psn()
                    k.tr(ps[:, 0:128], cin[:], C.ident[:])
                    ctf = ctr.next()
                    k.evac(ctf[:], ps[:, 0:128])
                    for g8 in range(8):
                        gg = g8 % 2
                        k.ts(CT[gg * 64:(gg + 1) * 64, d, ri, c4, g8, g8 * 16:(g8 + 1) * 16],
                             ctf[gg * 64:(gg + 1) * 64, g8 * 16:(g8 + 1) * 16], (1.0 if ri == 0 else -1.0), ALU.mult)
        dsk = k.sb([128, 4], name="s5D")
        k.dma("sp", dsk[:], I["s5_D"][0].r("(c p) -> p c", p=128))
        uT = k.sb([128, T], name="uT")
        yacc = k.sb([128, T], name="yacc")
        Tc = k.sb([128, S], name="Tc")
        Ts = k.sb([128, S], name="Tsn")
        magt = k.sb([128, S], name="magt")
        ones_s = k.sb([128, S], name="ones_s")
        k.memset(ones_s[:], 1.0)
        br_ = k.sb([128, S], name="btr")
        bi_ = k.sb([128, S], name="bti")
        wr_ = k.sb([128, S], BF16, name="wre")
        wi_ = k.sb([128, S], BF16, name="wim")
        tmr = k.ring(3, [128, 512], name="s5tmp")
        tbig = k.ring(2, [128, S], name="s5big")
        for c4 in range(4):
            k.dma("sp", uT[:], C.colsT[c4 * 128:(c4 + 1) * 128, :])
            first = True
            for d in range(2):
                for q in range(4):
                    j = c4 * 4 + q
                    cj = d * 16 + j
                    k.memset(Tc[:, 0:1], 1.0, "dve")
                    k.memset(Ts[:, 0:1], 0.0, "dve")
                    n = 1
                    lv = 0
                    while n < S:
                        pc, ps_ = PW[:, lv, 0, cj:cj + 1], PW[:, lv, 1, cj:cj + 1]
                        tb = tbig.next()
                        k.ts(tb[:, 0:n], Ts[:, 0:n], ps_, ALU.mult)
                        k.stt(Tc[:, n:2 * n], Tc[:, 0:n], pc, tb[:, 0:n], ALU.mult, ALU.subtract)
                        tb2 = tbig.next()
                        k.ts(tb2[:, 0:n], Tc[:, 0:n], ps_, ALU.mult)
                        k.stt(Ts[:, n:2 * n], Ts[:, 0:n], pc, tb2[:, 0:n], ALU.mult, ALU.add)
                        n *= 2
                        lv += 1
                    k.ts(magt[:], ones_s[:], mag[:, cj:cj + 1], ALU.mult)
                    for b in range(NB):
                        rev = (lambda v: V(v.t, v.ap[:, ::-1])) if d == 1 else (lambda v: v)
                        for t0 in range(0, S, 512):
                            tw = min(512, S - t0)
                            g0 = b * S + t0
                            psr = k.psn()
                            k.mm(psr[:, 0:tw], BT[:, d, 0, j, :], uT[:, g0:g0 + tw])
                            psi = k.psn()
                            k.mm(psi[:, 0:tw], BT[:, d, 1, j, :], uT[:, g0:g0 + tw])
                            if d == 0:
                                tc, tsn = Tc[:, t0:t0 + tw], Ts[:, t0:t0 + tw]
                            else:
                                tc = V(Tc, Tc.h[:, S - t0 - tw:S - t0][:, ::-1])
                                tsn = V(Ts, Ts.h[:, S - t0 - tw:S - t0][:, ::-1])
                            a1, a2_ = tmr.next(), tmr.next()
                            k.tt(a1[:, 0:tw], psr[:, 0:tw], tc, ALU.mult)
                            k.tt(a2_[:, 0:tw], psi[:, 0:tw], tsn, ALU.mult)
                            k.tt(br_[:, t0:t0 + tw], a1[:, 0:tw], a2_[:, 0:tw], ALU.add)
                            k.tt(a1[:, 0:tw], psi[:, 0:tw], tc, ALU.mult)
                            k.tt(a2_[:, 0:tw], psr[:, 0:tw], tsn, ALU.mult)
                            k.tt(bi_[:, t0:t0 + tw], a1[:, 0:tw], a2_[:, 0:tw], ALU.subtract)
                        k.scan(rev(br_[:]), magt[:], rev(br_[:]))
                        k.scan(rev(bi_[:]), magt[:], rev(bi_[:]))
                        tcf = Tc[:] if d == 0 else V(Tc, Tc.h[:, ::-1])
                        tsf = Ts[:] if d == 0 else V(Ts, Ts.h[:, ::-1])
                        b1, b2 = tbig.next(), tbig.next()
                        k.tt(b1[:], br_[:], tcf, ALU.mult)
                        k.tt(b2[:], bi_[:], tsf, ALU.mult)
                        k.tt(wr_[:], b1[:], b2[:], ALU.subtract)
                        b1, b2 = tbig.next(), tbig.next()
                        k.tt(b1[:], br_[:], tsf, ALU.mult)
                        k.tt(b2[:], bi_[:], tcf, ALU.mult)
                        k.tt(wi_[:], b1[:], b2[:], ALU.add)
                        for t0 in range(0, S, 512):
                            tw = min(512, S - t0)
                            g0 = b * S + t0
                            ps = k.psn()
                            n_mm = 0
                            for gg in range(2):
                                g8 = (2 * j + gg) % 8
                                for ri, wt in ((0, wr_), (1, wi_)):
                                    k.mm(ps[:, 0:tw], CT[:, d, ri, c4, g8, :],
                                         wt[:, t0:t0 + tw], start=(n_mm == 0), stop=(n_mm == 3))
                                    n_mm += 1
                            if first:
                                k.evac(yacc[:, g0:g0 + tw], ps[:, 0:tw])
                            else:
                                k.tt(yacc[:, g0:g0 + tw], yacc[:, g0:g0 + tw], ps[:, 0:tw], ALU.add)
                    first = False
            for t0 in range(0, T, 512):
                y = tmr.next()
                k.stt(y[:], uT[:, t0:t0 + 512], dsk[:, c4:c4 + 1], yacc[:, t0:t0 + 512], ALU.mult, ALU.add)
                x2 = tmr.next()
                k.tt(x2[:], y[:], y[:], ALU.mult)
                k.ts(x2[:], x2[:], 0.044715, ALU.mult, 1.0, ALU.add)
                k.tt(x2[:], x2[:], y[:], ALU.mult)
                k.act(x2[:], x2[:], AF.Sigmoid, scale=2.0 * math.sqrt(2.0 / math.pi))
                k.tt(y[:], y[:], x2[:], ALU.mult)
                k.dma("sp", sc["s5yT"][c4 * 128:(c4 + 1) * 128, t0:t0 + 512], y[:])
    TB = C.TB
    with k.scope():
        xin = k.sb([128, 4, TB], MMDT, name="gluin")
        gb = k.sb([128, 4], name="glub")
        k.dma("sp", gb[:], I["s5_glu_b"][0].r("(c p) -> p c", p=128))
        wring = k.ring(2, [128, 4, 512], MMDT, name="wglu")
        yr = k.ring(3, [128, 512], name="gluy")
        for tb0 in range(0, T, TB):
            k.dma("pool", xin[:], sc["s5yT"][:, tb0:tb0 + TB].r("(c p) t -> p c t", p=128))

            def epi(n, mw, t0, ps):
                yv = yr.next()
                k.dma("sp", yv[:], sc["s5yT"][n:n + 128, tb0 + t0:tb0 + t0 + 512])
                sg = yr.next()
                k.act(sg[:], ps[:], AF.Sigmoid, bias=gb[:, n // 128:n // 128 + 1])
                k.tt(yv[:], yv[:], sg[:], ALU.mult)
                k.dma("sp", C.mixT[n:n + 128, tb0 + t0:tb0 + t0 + 512], yv[:])
            dense(k, xin[:], 4, I["s5_glu_w"][0], 512, TB, epi, wring, 512)


def stage_mlstm(k, C):
    S, NB, T = C.S, C.NB, C.T
    I = C.inp
    sc = C.scr
    NI = S // 128
    with k.scope():
        wcv = k.sb([128, 8, 3], name="wcv")
        for kk_ in range(3):
            k.dma("sp", wcv[:, :, kk_], I["ml_conv_w"][0, kk_].r("(c p) -> p c", p=128))
        bcv = k.sb([128, 8], name="bcv")
        k.dma("sp", bcv[:], I["ml_conv_b"][0].r("(c p) -> p c", p=128))
        halo = k.ring(2, [128, S + 2], name="chalo")
        outr = k.ring(2, [128, S], name="cout")
        dwconv_silu(k, C, C.colsT, 512, 8, wcv[:], bcv[:], sc["xcT"], 0, halo, outr)
    with k.scope():
        m32 = k.sb([128, 32], name="m32")
        k.memset(m32[:], 1.0)
        k.aselect(m32[:], m32[:], [[-4, 32]], ALU.is_ge, 0.0, 0, 1)
        k.aselect(m32[:], m32[:], [[4, 32]], ALU.is_ge, 0.0, 3, -1)
        WB = k.sb([128, 3, 8, 128], name="WB")
        wl = k.sb([128, 3, 8, 4], name="wl")
        with k.nc.allow_non_contiguous_dma(reason="tiny weights"):
            for i, nm in enumerate(("ml_wq", "ml_wk", "ml_wv")):
                k.dma("sp", wl[:, i], I[nm][0].r("(c j) a d -> (j a) c d", j=32))
        k.ts(wl[:, 1], wl[:, 1], 128.0 ** -0.5, ALU.mult)
        for i in range(3):
            for c in range(8):
                k.tt(WB[:, i, c, :].r("p (j d) -> p j d", d=4), wl[:, i, c, :].us(1).bc([128, 32, 4]),
                     m32[:].us(2).bc([128, 32, 4]), ALU.mult)
        ib = k.sb([16, 1], name="ib")
        k.dma("sp", ib[:], I["ml_i_b"][0].r("d h -> (d h)").us(1))
        fbn = k.sb([16, 1], name="fbn")
        k.dma("sp", fbn[:], I["ml_f_b"][0].r("d h -> (d h)").us(1))
        k.ts(fbn[:], fbn[:], -1.0, ALU.mult)
        m01 = rowmask(k, 16, 8, "m01l")
        nwb = k.sb([128, 1024], name="nwb")
        k.dma("sp", nwb[:], I["ml_norm_w"][0].pb(128))
        qT = k.sb([128, 8, S], BF16, name="qT")
        kT = k.sb([128, 8, S], BF16, name="kT")
        vtok = k.sb([128, NI, 8, 130], BF16, name="vtok")
        k.memset(vtok[:, :, :, 128:129], 1.0)
        lmT = k.sb([128, NI, 16], name="lmT")
        nxT = k.sb([128, NI, 16], name="nxT")
        bsT = k.sb([128, NI, 16], name="bsT")
        xr = k.ring(2, [128, 512], name="mlx")
        vr = k.ring(1, [128, S], name="mlv")
        htr = k.ring(4, [128, 128], name="h0")
        hor = k.ring(3, [128, 128], name="ho")
        s1 = k.ring(6, [128, 1], name="s1")
        for b in range(NB):
            for c in range(8):
                vT = vr.next()
                for t0 in range(0, S, 512):
                    tw = min(512, S - t0)
                    g0 = b * S + t0
                    xc = xr.next()
                    k.dma("sp", xc[:, 0:tw], sc["xcT"][c * 128:(c + 1) * 128, g0:g0 + tw])
                    xm = xr.next()
                    k.dma("sp", xm[:, 0:tw], C.colsT[512 + c * 128:512 + (c + 1) * 128, g0:g0 + tw])
                    ps = k.psn()
                    k.mm(ps[:, 0:tw], WB[:, 0, c, :], xc[:, 0:tw])
                    k.evac(qT[:, c, t0:t0 + tw], ps[:, 0:tw])
                    ps = k.psn()
                    k.mm(ps[:, 0:tw], WB[:, 1, c, :], xc[:, 0:tw])
                    k.evac(kT[:, c, t0:t0 + tw], ps[:, 0:tw])
                    ps = k.psn()
                    k.mm(ps[:, 0:tw], WB[:, 2, c, :], xm[:, 0:tw])
                    k.evac(vT[:, t0:t0 + tw], ps[:, 0:tw])
                for i0 in range(0, NI, 4):
                    n = min(4, NI - i0)
                    ps = k.psn()
                    for i in range(i0, i0 + n):
                        k.tr(ps[:, (i - i0) * 128:(i - i0 + 1) * 128], vT[:, i * 128:(i + 1) * 128], C.ident[:])
                    k.evac(vtok[:, i0:i0 + n, c, 0:128], ps[:, 0:n * 128].r("p (i f) -> p i f", f=128))
            with k.scope():
                li = k.sb([16, S], name="li")
                lf = k.sb([16, S], name="lf")
                X = k.sb([16, S], name="Xl")
                onesr = k.sb([16, S], name="onesr")
                k.memset(onesr[:], 1.0)
                def tr16(src, dst):
                    for j0 in range(0, NI, 32):
                        ps = k.psn()
                        n = min(32, NI - j0)
                        for j in range(j0, j0 + n):
                            k.tr(ps[:, (j - j0) * 16:(j - j0 + 1) * 16], src[0:16, j * 128:(j + 1) * 128], C.ident[0:16, 0:16])
                        k.evac(dst[:, j0:j0 + n, :], ps[:, 0:n * 16].r("p (j r) -> p j r", r=16))
                k.dma("sp", li[:], C.colsT[2560:2576, b * S:(b + 1) * S])
                k.dma("sp", lf[:], C.colsT[2576:2592, b * S:(b + 1) * S])
                k.ts(li[:], li[:], ib[:, 0:1], ALU.add)
                tr16(li, lmT)
                tmp = li
                softplus_neg(k, lf[:], lf[:], fbn[:, 0:1], tmp[:])
                k.ts(lf[:], lf[:], -1.0, ALU.mult)
                k.scan(X[:], onesr[:], lf[:])
                k.scan(V(tmp, tmp.h[:, ::-1]), onesr[:], V(lf, lf.h[:, ::-1]))
                k.ts(X[:], X[:], m01[:, 0:1], ALU.mult)
                k.stt(X[:], tmp[:], m01[:, 1:2], X[:], ALU.mult, ALU.add)
                tr16(X, nxT)

                k.dma("sp", sc["xrow"][0:16, 0:S], X[:])
            k.ts(nxT[:], nxT[:], -1.0, ALU.mult)
            k.tt(bsT[:], lmT[:], nxT[:], ALU.add)
            hstate = {}

            def out_fn(h, J, d, acc):
                den = s1.next()
                k.act(den[:], acc[:, 128:129], AF.Abs)
                k.ts(den[:], den[:], 1.0, ALU.max)
                k.recip(den[:], den[:])
                if d == 0:
                    h0 = htr.next()
                    hstate[J] = h0
                    k.ts(h0[:], acc[:, 0:128], den[:, 0:1], ALU.mult)
                    return
                h0 = hstate[J]
                k.stt(h0[:], acc[:, 0:128], den[:, 0:1], h0[:], ALU.mult, ALU.add)
                mean = s1.next()
                k.reduce(mean[:], h0[:])
                k.ts(mean[:], mean[:], 1.0 / 128, ALU.mult)
                k.ts(h0[:], h0[:], mean[:, 0:1], ALU.subtract)
                ho = hor.next()
                var = s1.next()
                k.tt(ho[:], h0[:], h0[:], ALU.mult)
                k.reduce(var[:], ho[:])
                k.act(var[:], var[:], AF.Sqrt, bias=C.cst[:, 3:4], scale=1.0 / 128)
                k.recip(var[:], var[:])
                k.stt(ho[:], h0[:], var[:, 0:1], nwb[:, h * 128:(h + 1) * 128], ALU.mult, ALU.mult)
                g0 = b * S + J * 128
                k.dma("pool", sc["hmlTok"][g0:g0 + 128, h * 128:(h + 1) * 128], ho[:])

            decay_attention(
                k, C, 8, 1,
                lambda g, Ib: kT[:, g, Ib * 128:(Ib + 1) * 128],
                lambda g, t0, n: qT[:, g, t0:t0 + n],
                lambda h, Ib: vtok[:, Ib, h, 0:129], 129,
                None, bsT, nxT, lmT, lambda d, h: d * 8 + h, "den", out_fn, BF16, nst=1)
    with k.scope():
        skp = k.sb([128, 8], name="skp")
        k.dma("sp", skp[:], I["ml_skip"][0].r("(c p) -> p c", p=128))
        hr_ = k.ring(2, [128, 4, 1024], name="htk")
        orr = k.ring(2, [128, 8, 512], name="og")
        xr2 = k.ring(2, [128, 8, 512], name="xc2")
        for t0 in range(0, T, 512):
            ht = hr_.next()
            k.dma("sp", ht[:], sc["hmlTok"][t0:t0 + 512, :].r("(j p) f -> p j f", p=128))
            og = orr.next()
            k.dma("sp", og[:], C.colsT[1536:2560, t0:t0 + 512].r("(c p) t -> p c t", p=128))
            k.act(og[:], og[:], AF.Sigmoid)
            xc = xr2.next()
            k.dma("sp", xc[:], sc["xcT"][:, t0:t0 + 512].r("(c p) t -> p c t", p=128))
            for c in range(8):
                ps = k.psn()
                for j in range(4):
                    k.tr(ps[:, j * 128:(j + 1) * 128], ht[:, j, c * 128:(c + 1) * 128], C.ident[:])
                k.tt(og[:, c, :], og[:, c, :], ps[:], ALU.mult)
                k.stt(og[:, c, :], xc[:, c, :], skp[:, c:c + 1], og[:, c, :], ALU.mult, ALU.add)
            k.dma("sp", C.mixT[512:1536, t0:t0 + 512].r("(c p) t -> p c t", p=128), og[:])


def build(S, NB=2, stages=None, dbg=()):
    T = NB * S
    nc = bass.Bass("TRN2", target_bir_lowering=False)
    C = Ctx()
    C.S, C.NB, C.T = S, NB, T
    C.TB = min(1024, T)
    x = nc.dram_tensor("x", [NB, S, D], F32, kind="ExternalInput").ap()
    out = nc.dram_tensor("out", [NB, S, D], F32, kind="ExternalOutput").ap()
    C.x = Tl(x, "x").v()
    C.out = Tl(out, "out", acc=True).v()
    C.inp = {}
    for nm, shp in IN_SPECS:
        C.inp[nm] = Tl(nc.dram_tensor(nm, list(shp), F32, kind="ExternalInput").ap(), nm)
    with ExitStack() as es:
        es.enter_context(nc.allow_non_contiguous_dma(reason="small parameter layouts"))
        k = K(nc, es)

        def scr(name, shape):
            if name in dbg:
                return Tl(nc.dram_tensor(name, list(shape), F32, kind="ExternalOutput").ap(), name, acc=True)
            return k.dram(name, shape)
        C.hT = scr("hT", [D, T])
        C.colsT = scr("colsT", [AB_IN, T])
        C.mixT = scr("mixT", [1536, T])
        C.scr = {}
        for nm, shp in [("decT", [1024, T]), ("rwT", [1024, T]), ("aT", [512, T]), ("gT", [512, T]),
                        ("bonusT", [512, T]), ("rTok", [T, 512]), ("bTok", [T, 512]), ("kTok", [T, 512]),
                        ("vTok", [T, 512]), ("aTok", [T, 512]), ("lw0Tok", [T, 512]), ("lw1Tok", [T, 512]), ("saTok", [2 * T, 512]), ("ypTok", [2 * T, 512]),
                        ("xbcT", [1536, T]), ("ymbT", [1024, T]), ("s5yT", [512, T]), ("xcT", [1024, T]),
                        ("hmlTok", [T, 1024]), ("xrow", [32, S])]:
            C.scr[nm] = scr(nm, shp)
        stage_consts(k, C)
        st = stages or ["load", "in0", "rwkv", "mamba", "out0", "mlp0", "in1", "s5", "mlstm", "out1", "mlp1", "final"]
        for s_ in st:
            if s_ == "load":
                stage_load_x(k, C)
            elif s_ == "in0":
                stage_inproj(k, C, 0, C.inp["ab_w_in"][0], AB_IN, C.inp["norm_mix"][0].r("(c p) -> p c", p=128))
            elif s_ == "rwkv":
                stage_rwkv_prep(k, C)
                stage_rwkv_scan(k, C)
                stage_rwkv_post(k, C)
            elif s_ == "rwprep":
                stage_rwkv_prep(k, C)
            elif s_ == "rwscan":
                stage_rwkv_scan(k, C)
            elif s_ == "rwpost":
                stage_rwkv_post(k, C)
            elif s_ == "mamba":
                stage_mamba(k, C)
            elif s_ == "out0":
                stage_outproj(k, C, C.inp["ab_w_out"][0])
            elif s_ == "mlp0":
                stage_mlp(k, C, 0)
            elif s_ == "in1":
                stage_inproj(k, C, 1, C.inp["cd_w_in"][0], CD_IN, C.inp["norm_mix"][1].r("(c p) -> p c", p=128))
            elif s_ == "s5":
                stage_s5(k, C)
            elif s_ == "mlstm":
                stage_mlstm(k, C)
            elif s_ == "out1":
                stage_outproj(k, C, C.inp["cd_w_out"][0])
            elif s_ == "mlp1":
                stage_mlp(k, C, 1)
            elif s_ == "final":
                stage_final(k, C)
        k.finish()
    return nc


_NC_CACHE = {}


def kernel(**inputs):
    x = np.ascontiguousarray(np.asarray(inputs["x"], dtype=np.float32))
    B, S, _ = x.shape
    ncores = 8
    NB = B // ncores
    key = (S, NB)
    if key not in _NC_CACHE:
        _NC_CACHE[key] = build(S, NB)
    nc = _NC_CACHE[key]
    shared = {nm: np.ascontiguousarray(np.asarray(inputs[nm], dtype=np.float32)) for nm, _ in IN_SPECS}
    in_maps = []
    for i in range(ncores):
        m = dict(shared)
        m["x"] = x[i * NB:(i + 1) * NB]
        in_maps.append(m)
    res = run_bass_kernel_spmd(nc, in_maps, core_ids=list(range(ncores)))
    return np.concatenate([r["out"] for r in res.results], axis=0).astype(np.float32)
```
